# Optimizing a Trainium2 kernel written in Bass

```python
import jax, jax.numpy as jnp
from jax import lax
import numpy as np

D_MODEL = 1024
BATCH = 16
SEQ = 4096
DEPTH = 2

HEAD_DIM = 64
N_HEADS = D_MODEL // HEAD_DIM
DILATED_GROUPS = ((128, 1), (512, 4), (2048, 16))
N_GROUPS = len(DILATED_GROUPS)
BAND_BLOCK = 128
ROT_DIM = HEAD_DIM // 4
ROPE_THETA = 500000.0
FOX_BLOCK = 128
D_FF = 2816
N_A_LAYERS = DEPTH // 2
N_B_LAYERS = DEPTH - N_A_LAYERS
HD = N_HEADS * HEAD_DIM
EPS = 1e-6

kernel_name = "yoco_dilated_fox_macaron_trunk"


def rms_norm(x, g):
    xf = x.astype(jnp.float32)
    y = xf * lax.rsqrt(jnp.mean(xf * xf, axis=-1, keepdims=True) + EPS)
    return (y * g.astype(jnp.float32)).astype(x.dtype)


def swiglu(x, w_in, w_out):
    gate, up = jnp.split(x @ w_in, 2, axis=-1)
    return (jax.nn.silu(gate) * up) @ w_out


def rope_partial(x, positions):
    half = ROT_DIM // 2
    inv_freq = ROPE_THETA ** (-jnp.arange(0, ROT_DIM, 2, dtype=jnp.float32) / ROT_DIM)
    ang = positions.astype(jnp.float32)[..., None] * inv_freq
    cos, sin = jnp.cos(ang)[:, :, None, :], jnp.sin(ang)[:, :, None, :]
    xr = x[..., :ROT_DIM].astype(jnp.float32)
    x1, x2 = xr[..., :half], xr[..., half:]
    rot = jnp.concatenate([x1 * cos - x2 * sin, x2 * cos + x1 * sin], axis=-1)
    return jnp.concatenate([rot.astype(x.dtype), x[..., ROT_DIM:]], axis=-1)


def dilated_band_attention(q, k, v, window, dilation):
    b, s, h, dh = q.shape
    n_steps = window // dilation
    span = dilation * BAND_BLOCK
    s_pad = -(-s // span) * span
    seq_len = s_pad // dilation
    nb = seq_len // BAND_BLOCK
    pad = ((0, 0), (0, s_pad - s), (0, 0), (0, 0))

    def to_blocks(t):
        t = jnp.pad(t, pad).reshape(b, seq_len, dilation, h, dh).transpose(0, 2, 1, 3, 4)
        return t.reshape(b, dilation, nb, BAND_BLOCK, h, dh)

    def with_prev(t):
        prev = jnp.pad(t, ((0, 0), (0, 0), (1, 0), (0, 0), (0, 0), (0, 0)))[:, :, :-1]
        return jnp.concatenate([prev, t], axis=3)

    qb = to_blocks(q)
    kk = with_prev(to_blocks(k))
    vv = with_prev(to_blocks(v))
    scores = jnp.einsum('brnqhd,brnkhd->brnhqk', qb, kk).astype(jnp.float32) * (dh ** -0.5)
    qi = jnp.arange(BAND_BLOCK)[:, None]
    kj = jnp.arange(2 * BAND_BLOCK)[None, :]
    dist = qi + BAND_BLOCK - kj
    blk = jnp.arange(nb)[:, None, None]
    valid = (dist >= 0) & (dist <= n_steps) & ((blk > 0) | (kj >= BAND_BLOCK))
    scores = jnp.where(valid[:, None], scores, -jnp.inf)
    lse = jax.nn.logsumexp(scores, axis=-1)
    probs = jnp.exp(scores - lse[..., None])
    out = jnp.einsum('brnhqk,brnkhd->brnqhd', probs.astype(v.dtype), vv)
    out = out.transpose(0, 2, 3, 1, 4, 5).reshape(b, s_pad, h, dh)[:, :s]
    lse = lse.transpose(0, 2, 4, 1, 3).reshape(b, s_pad, h)[:, :s]
    return out, lse


def dilated_mixture_mixer(hn, positions, w_qkv, q_norm, k_norm, w_o):
    b, s, _ = hn.shape
    qkv = (hn @ w_qkv).reshape(b, s, N_GROUPS, 3, N_HEADS, HEAD_DIM)
    outs, lses = [], []
    for g, (window, dilation) in enumerate(DILATED_GROUPS):
        q = rope_partial(rms_norm(qkv[:, :, g, 0], q_norm[g]), positions)
        k = rope_partial(rms_norm(qkv[:, :, g, 1], k_norm[g]), positions)
        o, lse = dilated_band_attention(q, k, qkv[:, :, g, 2], window, dilation)
        outs.append(o.astype(jnp.float32))
        lses.append(lse)
    alpha = jax.nn.softmax(jnp.stack(lses, axis=0), axis=0)
    mixed = jnp.sum(alpha[..., None] * jnp.stack(outs, axis=0), axis=0).astype(hn.dtype)
    return mixed.reshape(b, s, HD) @ w_o


def shared_kv(hn, w_kv, b_f, k_norm):
    b, s, _ = hn.shape
    proj = hn @ w_kv
    k = rms_norm(proj[..., :HD].reshape(b, s, N_HEADS, HEAD_DIM), k_norm)
    v = proj[..., HD:2 * HD].reshape(b, s, N_HEADS, HEAD_DIM)
    log_f = jax.nn.log_sigmoid(proj[..., 2 * HD:].astype(jnp.float32) + b_f.astype(jnp.float32))
    cum = jnp.cumsum(log_f, axis=1)
    return k, v, cum


def forgetting_attention(hn, k, v, cum, w_q, q_norm, w_o):
    b, s, _ = hn.shape
    q = rms_norm((hn @ w_q).reshape(b, s, N_HEADS, HEAD_DIM), q_norm)
    nb = s // FOX_BLOCK
    q_blocks = q.reshape(b, nb, FOX_BLOCK, N_HEADS, HEAD_DIM).transpose(1, 0, 2, 3, 4)
    c_blocks = cum.reshape(b, nb, FOX_BLOCK, N_HEADS).transpose(1, 0, 2, 3)
    ck = cum.transpose(0, 2, 1)[:, :, None, :]
    key_pos = jnp.arange(s)
    scale = HEAD_DIM ** -0.5

    def block(args):
        qb, cb, bi = args
        logits = jnp.einsum('bqhd,bshd->bhqs', qb, k).astype(jnp.float32) * scale
        logits = logits + (cb.transpose(0, 2, 1)[..., None] - ck)
        qpos = bi * FOX_BLOCK + jnp.arange(FOX_BLOCK)
        logits = jnp.where(key_pos[None, :] <= qpos[:, None], logits, -jnp.inf)
        p = jax.nn.softmax(logits, axis=-1)
        return jnp.einsum('bhqs,bshd->bqhd', p.astype(v.dtype), v)

    o = lax.map(block, (q_blocks, c_blocks, jnp.arange(nb)))
    o = o.transpose(1, 0, 2, 3, 4).reshape(b, s, HD)
    return o @ w_o


def setup_inputs(seed: int = 0) -> dict:
    key = jax.random.key(seed)
    ks = jax.random.split(key, 20)
    f32 = jnp.float32

    def nrm(k, shape, fan_in):
        return jax.random.normal(k, shape, f32) * (fan_in ** -0.5)

    def gain(k, shape):
        return 1.0 + 0.05 * jax.random.normal(k, shape, f32)

    x = jax.random.normal(ks[0], (BATCH, SEQ, D_MODEL), f32)
    offset = jax.random.randint(ks[1], (BATCH, 1), 0, 1024, dtype=jnp.int32)
    positions = (jnp.arange(SEQ, dtype=jnp.int32)[None, :] + offset).astype(jnp.int32)
    return {
        "x": x,
        "positions": positions,
        "ffn_norm": gain(ks[2], (DEPTH, 2, D_MODEL)),
        "ffn_w_in": nrm(ks[3], (DEPTH, 2, D_MODEL, 2 * D_FF), D_MODEL),
        "ffn_w_out": nrm(ks[4], (DEPTH, 2, D_FF, D_MODEL), D_FF),
        "mix_norm": gain(ks[5], (DEPTH, D_MODEL)),
        "a_w_qkv": nrm(ks[6], (N_A_LAYERS, D_MODEL, N_GROUPS * 3 * HD), D_MODEL),
        "a_q_norm": gain(ks[7], (N_A_LAYERS, N_GROUPS, HEAD_DIM)),
        "a_k_norm": gain(ks[8], (N_A_LAYERS, N_GROUPS, HEAD_DIM)),
        "a_w_o": nrm(ks[9], (N_A_LAYERS, HD, D_MODEL), HD),
        "kv_norm": gain(ks[10], (D_MODEL,)),
        "kv_w": nrm(ks[11], (D_MODEL, 2 * HD + N_HEADS), D_MODEL),
        "kv_b_f": 0.1 * jax.random.normal(ks[12], (N_HEADS,), f32),
        "kv_k_norm": gain(ks[13], (HEAD_DIM,)),
        "b_w_q": nrm(ks[14], (N_B_LAYERS, D_MODEL, HD), D_MODEL),
        "b_q_norm": gain(ks[15], (N_B_LAYERS, HEAD_DIM)),
        "b_w_o": nrm(ks[16], (N_B_LAYERS, HD, D_MODEL), HD),
    }


def reference(x, positions, ffn_norm, ffn_w_in, ffn_w_out, mix_norm, a_w_qkv, a_q_norm, a_k_norm, a_w_o,
              kv_norm, kv_w, kv_b_f, kv_k_norm, b_w_q, b_q_norm, b_w_o):
    h = x
    k_sh = v_sh = cum_sh = None
    for layer in range(DEPTH):
        if layer == N_A_LAYERS:
            k_sh, v_sh, cum_sh = shared_kv(rms_norm(h, kv_norm), kv_w, kv_b_f, kv_k_norm)
        h = h + 0.5 * swiglu(rms_norm(h, ffn_norm[layer, 0]), ffn_w_in[layer, 0], ffn_w_out[layer, 0])
        hn = rms_norm(h, mix_norm[layer])
        if layer < N_A_LAYERS:
            h = h + dilated_mixture_mixer(hn, positions, a_w_qkv[layer], a_q_norm[layer],
                                          a_k_norm[layer], a_w_o[layer])
        else:
            j = layer - N_A_LAYERS
            h = h + forgetting_attention(hn, k_sh, v_sh, cum_sh, b_w_q[j], b_q_norm[j], b_w_o[j])
        h = h + 0.5 * swiglu(rms_norm(h, ffn_norm[layer, 1]), ffn_w_in[layer, 1], ffn_w_out[layer, 1])
    return h
```

```python
import contextlib
import math
import numpy as np
import concourse.bass as bass
import concourse.mybir as mybir
from concourse.bass_utils import run_bass_kernel_spmd

F32 = mybir.dt.float32
BF16 = mybir.dt.bfloat16
I32 = mybir.dt.int32
AF = mybir.ActivationFunctionType
ALU = mybir.AluOpType

S = 4096
D = 1024
DFF = 2816
NSEQ = 2
NCORES = 8
EPS = 1e-6
NEG = -30000.0
DIL = (1, 4, 16)

C_GF = 0
C_GM = 32
C_GKV = 48
C_AQ = 56
C_AK = 59
C_KVK = 62
C_BQ = 63
C_INVF = 64
C_BF = 65
C_MA = 66
C_MB = 67
NCST = 68
M_ID = 0
M_ONES = 128
M_BD = 256
M_PM = 384
M_BAND = 512
M_CAUS = 768
NMAT = 896


class Buf:
    __slots__ = ("w", "r")

    def __init__(self):
        self.w = None
        self.r = {}


class Eng:
    def __init__(self, name, sem):
        self.name = name
        self.sem = sem
        self.n = 0
        self.waited = {}
        self.ops = []
        self.pending = False


class Slot:
    def __init__(self, name, sem):
        self.name = name
        self.sem = sem
        self.n = 0


class Sched:
    SAME_ENGINE = ("act", "dve", "pool")

    def __init__(self, nc, es):
        self.nc = nc
        self.es = es
        self.eng = {}
        for name in ("pe", "act", "dve", "pool", "sp"):
            self.eng[name] = Eng(name, es.enter_context(nc.semaphore("e_" + name)))
        self.slots = {}
        self.semof = {e.name: e.sem for e in self.eng.values()}

    def slot(self, name):
        if name not in self.slots:
            s = Slot(name, self.es.enter_context(self.nc.semaphore("d_" + name)))
            self.slots[name] = s
            self.semof[name] = s.sem
        return self.slots[name]

    def _waits(self, eng, reads, writes):
        need = {}

        def add(tok):
            if tok is None:
                return
            k, v = tok
            if k == eng.name:
                if eng.name not in self.SAME_ENGINE or v > eng.n:
                    return
            if eng.waited.get(k, 0) >= v:
                return
            if need.get(k, 0) < v:
                need[k] = v

        for b in reads:
            add(b.w)
        for b in writes:
            add(b.w)
            for k, v in b.r.items():
                add((k, v))
        for k, v in need.items():
            eng.waited[k] = v
        return [(self.semof[k], v) for k, v in need.items()]

    def _commit(self, tok, reads, writes):
        k, v = tok
        for b in reads:
            if b.r.get(k, 0) < v:
                b.r[k] = v
        for b in writes:
            b.w = tok
            b.r = {}

    def op(self, e, fn, reads=(), writes=(), sig=True):
        eng = self.eng[e]
        waits = self._waits(eng, reads, writes)
        tok = (eng.name, eng.n + 1)
        eng.ops.append((waits, fn, eng.sem if sig else None, 1))
        if sig:
            eng.n += 1
            eng.pending = False
        else:
            eng.pending = True
        self._commit(tok, reads, writes)
        return tok

    def dma(self, q, slotname, out, in_, reads=(), writes=()):
        eng = self.eng[q]
        sl = self.slot(slotname)
        waits = self._waits(eng, reads, writes)
        sl.n += 1
        tok = (sl.name, 16 * sl.n)
        eng.ops.append((waits, (lambda h, o=out, i=in_: h.dma_start(out=o, in_=i)), sl.sem, 16))
        self._commit(tok, reads, writes)
        return tok

    def barrier(self):
        toks = [(e.name, e.n) for e in self.eng.values() if e.n > 0]
        toks += [(s.name, 16 * s.n) for s in self.slots.values() if s.n > 0]
        for e in self.eng.values():
            assert not e.pending, e.name
            need = []
            for k, v in toks:
                if k == e.name:
                    continue
                if e.waited.get(k, 0) >= v:
                    continue
                e.waited[k] = v
                need.append((self.semof[k], v))
            if need:
                e.ops.append((need, None, None, 0))

    def emit(self, block):
        hmap = {"pe": block.tensor, "act": block.scalar, "dve": block.vector,
                "pool": block.gpsimd, "sp": block.sync}
        for name, deco in hmap.items():
            eng = self.eng[name]
            assert not eng.pending, name

            def body(h, eng=eng):
                for waits, fn, sem, inc in eng.ops:
                    for s, v in waits:
                        h.wait_ge(s, v)
                    if fn is None:
                        continue
                    inst = fn(h)
                    if sem is not None:
                        inst.then_inc(sem, inc)
            deco(body)


def build_program(stop_after=99):
    nc = bass.Bass("TRN2", target_bir_lowering=False)
    dt = nc.dram_tensor
    xT = dt("xT", [NSEQ, D, S], F32, kind="ExternalInput").ap()
    posb = dt("posb", [NSEQ, 128, S], I32, kind="ExternalInput").ap()
    cst_d = dt("cst", [128, NCST], F32, kind="ExternalInput").ap()
    cmat_d = dt("cmat", [128, NMAT], F32, kind="ExternalInput").ap()
    w_in = dt("ffn_w_in", [2, 2, D, 2 * DFF], F32, kind="ExternalInput").ap()
    w_out = dt("ffn_w_out", [2, 2, DFF, D], F32, kind="ExternalInput").ap()
    a_qkv = dt("a_w_qkv", [D, 9216], F32, kind="ExternalInput").ap()
    a_wo = dt("a_w_o", [D, D], F32, kind="ExternalInput").ap()
    kv_w = dt("kv_w", [D, 2064], F32, kind="ExternalInput").ap()
    b_wq = dt("b_w_q", [D, D], F32, kind="ExternalInput").ap()
    b_wo = dt("b_w_o", [D, D], F32, kind="ExternalInput").ap()
    yT = dt("yT", [NSEQ, D, S], F32, kind="ExternalOutput").ap()
    qk_scr = dt("qk_scr", [NSEQ, 3, 2, 8, 128, S], BF16, kind="Internal").ap()
    v_scr = dt("v_scr", [NSEQ, 3, 32, 128, D], BF16, kind="Internal").ap()
    o_scr = dt("o_scr", [NSEQ, D, S], BF16, kind="Internal").ap()
    kB_scr = dt("kB_scr", [NSEQ, 8, 128, S], BF16, kind="Internal").ap()
    vB_scr = dt("vB_scr", [NSEQ, 32, 128, D], BF16, kind="Internal").ap()
    qB_scr = dt("qB_scr", [NSEQ, 8, 128, S], BF16, kind="Internal").ap()
    lf_scr = dt("lf_scr", [NSEQ, 16, S], F32, kind="Internal").ap()
    qb_scr = dt("qb_scr", [NSEQ, 16, 6, S], BF16, kind="Internal").ap()
    kb_scr = dt("kb_scr", [NSEQ, 16, 6, S], BF16, kind="Internal").ap()
    winbf = dt("winbf", [2, 2, 11, 128, 8 * 512], BF16, kind="Internal").ap()
    woutbf = dt("woutbf", [2, 2, 4, 128, 22 * 256], BF16, kind="Internal").ap()

    with contextlib.ExitStack() as es:
        sc = Sched(nc, es)
        uid = [0]

        def sb(es_, shape, dtype, name="t"):
            uid[0] += 1
            return es_.enter_context(nc.sbuf_tensor("%s%d" % (name, uid[0]), shape, dtype))

        def ps(es_, name="p", shape=(128, 512)):
            uid[0] += 1
            return es_.enter_context(nc.psum_tensor("%s%d" % (name, uid[0]), list(shape), F32))

        cst = sb(es, [128, NCST], F32, "cst")
        cmat = sb(es, [128, NMAT], BF16, "cmat")
        nbf = sb(es, [128, 1], F32, "nbf")
        B_cst = Buf()
        B_cmat = Buf()
        B_nbf = Buf()
        sc.dma("sp", "cst", cst[:], cst_d, writes=[B_cst])
        sc.dma("pool", "cmat", cmat[:], cmat_d, writes=[B_cmat])
        sc.op("dve", lambda h: h.tensor_scalar(out=nbf[0:16, :], in0=cst[0:16, C_BF:C_BF + 1],
                                               scalar1=-1.0, scalar2=None, op0=ALU.mult),
              reads=[B_cst], writes=[B_nbf])
        ident = cmat[:, M_ID:M_ID + 128]
        ones_bf = cmat[:, M_ONES:M_ONES + 128]
        bd_ones = cmat[:, M_BD:M_BD + 128]
        pm = cmat[:, M_PM:M_PM + 128]
        band = cmat[:, M_BAND:M_BAND + 256]
        caus = cmat[:, M_CAUS:M_CAUS + 128]

        B_y = [[Buf() for _ in range(8)] for _ in range(NSEQ)]
        B_qk = {}
        B_v = {}
        B_o = [[Buf() for _ in range(8)] for _ in range(NSEQ)]
        B_misc = {}

        def gb(dct, key):
            if key not in dct:
                dct[key] = Buf()
            return dct[key]

        B_wbf = {}
        state = {"src": xT}

        def mm(out, lhsT, rhs, start, stop, reads, writes, sig):
            sc.op("pe", lambda h: h.matmul(out, lhsT=lhsT, rhs=rhs, start=start, stop=stop,
                                           skip_group_check=True),
                  reads=reads, writes=writes, sig=sig)

        class NormScratch:
            def __init__(self, es_, ss=None, B_ss=None, nhs=2):
                self.hs = [sb(es_, [128, 8, 512], F32, "hs") for _ in range(nhs)]
                self.B_hs = [Buf() for _ in range(nhs)]
                self.sqb = sb(es_, [128, 8, 512], BF16, "sqb")
                self.B_sqb = Buf()
                self.lt = sb(es_, [128, 512], F32, "lt")
                self.B_lt = Buf()
                self.rstd = sb(es_, [128, 512], F32, "rstd")
                self.B_rstd = Buf()
                self.ss = ss if ss is not None else ps(es_, "ss")
                self.B_ss = B_ss if B_ss is not None else Buf()

        def load_h(ns, i, src, b, t512):
            sc.dma("sp", "hs%d" % i, ns.hs[i][:],
                   src[b][:, t512 * 512:(t512 + 1) * 512].rearrange("(k p) t -> p k t", p=128),
                   reads=[B_y[b][t512]], writes=[ns.B_hs[i]])

        def norm_a(ns, i):
            hs = ns.hs[i]
            sc.op("act", lambda h: h.activation(out=ns.sqb[:], in_=hs[:], func=AF.Square),
                  reads=[ns.B_hs[i]], writes=[ns.B_sqb])

        def norm_b1(ns, i):
            for k in range(8):
                mm(ns.ss[:], ones_bf, ns.sqb[:, k, :], k == 0, k == 7,
                   reads=[ns.B_sqb, B_cmat], writes=[ns.B_ss], sig=(k == 7))

        def norm_b2(ns, i):
            sc.op("act", lambda h: h.activation(out=ns.lt[:], in_=ns.ss[:], func=AF.Ln,
                                                scale=1.0 / D, bias=EPS),
                  reads=[ns.B_ss], writes=[ns.B_lt])
            sc.op("act", lambda h: h.activation(out=ns.rstd[:], in_=ns.lt[:], func=AF.Exp, scale=-0.5),
                  reads=[ns.B_lt], writes=[ns.B_rstd])

        def norm_c(ns, i, gcol, dest, B_dest, ks=range(8)):
            hs = ns.hs[i]
            for k in ks:
                sc.op("dve", lambda h, k=k: h.scalar_tensor_tensor(
                    out=dest[:, k, :], in0=hs[:, k, :], scalar=cst[:, gcol + k:gcol + k + 1],
                    in1=ns.rstd[:], op0=ALU.mult, op1=ALU.mult),
                    reads=[ns.B_hs[i], ns.B_rstd, B_cst], writes=[B_dest])

        def norm_sub(ns, i, gcol, dest, B_dest):
            norm_a(ns, i)
            norm_b1(ns, i)
            norm_b2(ns, i)
            norm_c(ns, i, gcol, dest, B_dest)

        def phase_ffn(l, i):
            src = state["src"]
            win_d = w_in[l, i].rearrange("(k p) n -> p k n", p=128)
            wout_d = w_out[l, i].rearrange("(j p) n -> p j n", p=128)
            gcol = C_GF + (l * 2 + i) * 8
            with contextlib.ExitStack() as pes:
                ns = NormScratch(pes, nhs=3)
                xn = [sb(pes, [128, 8, 1024], BF16, "xn") for _ in range(2)]
                B_xn = [[Buf(), Buf()], [Buf(), Buf()]]
                act = sb(pes, [128, 22, 1024], BF16, "act")
                B_act = [[Buf(), Buf()] for _ in range(22)]
                win = [sb(pes, [128, 8, 512], BF16, "win") for _ in range(3)]
                B_win = [[Buf(), Buf()] for _ in range(3)]
                wout = [sb(pes, [128, 22, 256], BF16, "wout") for _ in range(2)]
                B_wout = [Buf(), Buf()]
                sg = [sb(pes, [128, 512], F32, "sg") for _ in range(2)]
                B_sg = [Buf(), Buf()]
                gps = [ps(pes, "g") for _ in range(2)]
                ups = [ps(pes, "u") for _ in range(2)]
                ops_ = [ps(pes, "o") for _ in range(2)]
                B_g = [Buf(), Buf()]
                B_u = [Buf(), Buf()]
                B_o2 = [Buf(), Buf()]
                wc = 0
                oc = 0
                gc = 0
                uc = 0
                def norm_steps(tile):
                    b, t0 = tile // 4, (tile % 4) * 2
                    xnb = xn[tile % 2]
                    st = []
                    for s_ in range(2):
                        dest, Bd = xnb[:, :, s_ * 512:(s_ + 1) * 512], B_xn[tile % 2][s_]
                        st.append(lambda s_=s_, b=b, t0=t0: (load_h(ns, 2, src, b, t0 + s_), norm_a(ns, 2)))
                        st.append(lambda dest=dest, Bd=Bd: (norm_b1(ns, 2), norm_b2(ns, 2), norm_c(ns, 2, gcol, dest, Bd)))
                    return st

                for f in norm_steps(0):
                    f()
                for tile in range(NSEQ * 4):
                    b, t0 = tile // 4, (tile % 4) * 2
                    xnb = xn[tile % 2]
                    nxt = norm_steps(tile + 1) if tile + 1 < NSEQ * 4 else []
                    for pc in range(11):
                        if 4 <= pc < 4 + len(nxt):
                            nxt[pc - 4]()
                        sl = wc % 3
                        wc += 1
                        if (l, i, "in", pc) in B_wbf:
                            sc.dma("pool", "win%da" % sl, win[sl][:],
                                   winbf[l, i, pc].rearrange("p (k n) -> p k n", k=8),
                                   reads=[B_wbf[(l, i, "in", pc)]], writes=[B_win[sl][0], B_win[sl][1]])
                        else:
                            sc.dma("pool", "win%da" % sl, win[sl][:, :, 0:256],
                                   win_d[:, :, pc * 256:(pc + 1) * 256], writes=[B_win[sl][0]])
                            sc.dma("pool", "win%db" % sl, win[sl][:, :, 256:512],
                                   win_d[:, :, DFF + pc * 256:DFF + (pc + 1) * 256], writes=[B_win[sl][1]])
                            B_wbf[(l, i, "in", pc)] = Buf()
                            sc.dma("sp", "wsv%d" % sl, winbf[l, i, pc].rearrange("p (k n) -> p k n", k=8), win[sl][:],
                                   reads=B_win[sl], writes=[B_wbf[(l, i, "in", pc)]])
                        for jj in range(2):
                            j = pc * 2 + jj
                            for s_ in range(2):
                                gi = gc % 2
                                gc += 1
                                rhs_s = slice(s_ * 512, (s_ + 1) * 512)
                                for k in range(8):
                                    mm(gps[gi][:], win[sl][:, k, jj * 128:(jj + 1) * 128], xnb[:, k, rhs_s],
                                       k == 0, k == 7, reads=[B_win[sl][0], B_xn[tile % 2][s_]],
                                       writes=[B_g[gi]], sig=(k == 7))
                                for k in range(8):
                                    mm(ups[gi][:], win[sl][:, k, 256 + jj * 128:256 + (jj + 1) * 128],
                                       xnb[:, k, rhs_s], k == 0, k == 7,
                                       reads=[B_win[sl][1], B_xn[tile % 2][s_]], writes=[B_u[gi]], sig=(k == 7))
                                sc.op("act", lambda h, gi=gi: h.activation(out=sg[gi][:], in_=gps[gi][:], func=AF.Silu),
                                      reads=[B_g[gi]], writes=[B_sg[gi]])
                                sc.op("dve", lambda h, gi=gi, j=j, rhs_s=rhs_s: h.tensor_tensor(
                                    out=act[:, j, rhs_s], in0=sg[gi][:], in1=ups[gi][:], op=ALU.mult),
                                    reads=[B_sg[gi], B_u[gi]], writes=[B_act[j][s_]])
                    for s_ in range(2):
                        load_h(ns, s_, src, b, t0 + s_)
                    for np_ in range(4):
                        sl = oc % 2
                        oc += 1
                        if (l, i, "out", np_) in B_wbf:
                            sc.dma("pool", "wout%d" % sl, wout[sl][:],
                                   woutbf[l, i, np_].rearrange("p (j n) -> p j n", j=22),
                                   reads=[B_wbf[(l, i, "out", np_)]], writes=[B_wout[sl]])
                        else:
                            sc.dma("pool", "wout%d" % sl, wout[sl][:], wout_d[:, :, np_ * 256:(np_ + 1) * 256],
                                   writes=[B_wout[sl]])
                            B_wbf[(l, i, "out", np_)] = Buf()
                            sc.dma("sp", "wso%d" % sl, woutbf[l, i, np_].rearrange("p (j n) -> p j n", j=22), wout[sl][:],
                                   reads=[B_wout[sl]], writes=[B_wbf[(l, i, "out", np_)]])
                        for nn in range(2):
                            n = np_ * 2 + nn
                            for s_ in range(2):
                                oi = uc % 2
                                uc += 1
                                for j in range(22):
                                    mm(ops_[oi][:], wout[sl][:, j, nn * 128:(nn + 1) * 128],
                                       act[:, j, s_ * 512:(s_ + 1) * 512], j == 0, j == 21,
                                       reads=[B_wout[sl], B_act[j][s_]], writes=[B_o2[oi]], sig=(j == 21))
                                sc.op("dve", lambda h, oi=oi, n=n, s_=s_: h.scalar_tensor_tensor(
                                    out=ns.hs[s_][:, n, :], in0=ops_[oi][:], scalar=0.5, in1=ns.hs[s_][:, n, :],
                                    op0=ALU.mult, op1=ALU.add),
                                    reads=[B_o2[oi], ns.B_hs[s_]], writes=[ns.B_hs[s_]])
                    for s_ in range(2):
                        sc.dma("sp", "hs%d" % s_, yT[b][:, (t0 + s_) * 512:(t0 + s_ + 1) * 512].rearrange(
                            "(k p) t -> p k t", p=128), ns.hs[s_][:], reads=[ns.B_hs[s_]], writes=[B_y[b][t0 + s_]])
                sc.barrier()
            state["src"] = yT

        def run_pipeline(units, extras=None):
            extras = sorted(extras or [], key=lambda t: t[0])
            ei = 0
            maxst = max(len(u) for u in units) if units else 0
            for i in range(len(units) + max(maxst - 1, 0)):
                while ei < len(extras) and extras[ei][0] <= i:
                    extras[ei][1]()
                    ei += 1
                for s in range(maxst):
                    ui_ = i - s
                    if 0 <= ui_ < len(units) and s < len(units[ui_]):
                        units[ui_][s]()
            while ei < len(extras):
                extras[ei][1]()
                ei += 1

        def phase_proj(kind):
            src = state["src"]
            with contextlib.ExitStack() as pes:
                NBIG = 4
                big = [ps(pes, "big") for _ in range(NBIG)]
                B_big = [Buf() for _ in range(NBIG)]
                ns = NormScratch(pes, ss=big[NBIG - 1], B_ss=B_big[NBIG - 1])
                hn = sb(pes, [128, 8, 2048], BF16, "hn")
                B_hn = [Buf() for _ in range(4)]
                NWQ = 3
                wq = [sb(pes, [128, 8, 128], BF16, "wq") for _ in range(NWQ)]
                B_wq = [Buf() for _ in range(NWQ)]
                wv = [sb(pes, [128, 8, 512], BF16, "wv") for _ in range(2)]
                B_wv = [Buf(), Buf()]

                def two(shape, dtype, name):
                    return [sb(pes, shape, dtype, name) for _ in range(2)], [Buf(), Buf()]
                sqs, B_sqs = two([128, 512], BF16, "sqs")
                lt2, B_lt2 = two([128, 512], F32, "lt2")
                rs2, B_rs2 = two([128, 512], F32, "rs2")
                qn, B_qn = two([128, 512], BF16, "qn")
                t1, B_t1 = two([128, 512], F32, "t1")
                t2, B_t2 = two([128, 512], F32, "t2")
                qst, B_qst = two([128, 2048], BF16, "qst")
                vst, B_vst = two([128, 512], BF16, "vst")
                ss2 = [ps(pes, "ss2") for _ in range(2)]
                B_ss2 = [Buf(), Buf()]
                sw = [ps(pes, "sw") for _ in range(2)]
                B_sw = [Buf(), Buf()]
                if kind == "A":
                    ctab = sb(pes, [128, 2048], F32, "ctab")
                    stab = sb(pes, [128, 2048], F32, "stab")
                    pi = sb(pes, [128, 512], I32, "pi")
                    u = sb(pes, [128, 512], F32, "u")
                    ui = sb(pes, [128, 512], I32, "ui")
                    uf = sb(pes, [128, 512], F32, "uf")
                    tmpf = sb(pes, [128, 512], F32, "tmpf")
                    B_ct, B_st, B_pi, B_u, B_ui, B_uf, B_tmpf = Buf(), Buf(), Buf(), Buf(), Buf(), Buf(), Buf()
                if kind == "KV":
                    wf = sb(pes, [128, 8, 16], BF16, "wf")
                    B_wf = Buf()
                    lst = sb(pes, [16, 2048], F32, "lst")
                    B_lst = Buf()
                    ef, B_ef = two([16, 512], F32, "ef")
                    sc.dma("pool", "wf", wf[:], kv_w.rearrange("(k p) n -> p k n", p=128)[:, :, 2048:2064],
                           writes=[B_wf])
                gmcol = {"A": C_GM, "KV": C_GKV, "QB": C_GM + 8}[kind]
                cnt = {"u": 0, "wv": 0, "vs": 0, "qs": 0, "wq": 0}

                def trig(dst, B_dst, shift):
                    sc.op("dve", lambda h: h.tensor_scalar(out=uf[:], in0=u[:], scalar1=float(shift), scalar2=None,
                                                           op0=ALU.add), reads=[B_u], writes=[B_uf])
                    sc.op("dve", lambda h: h.tensor_copy(out=ui[:], in_=uf[:]), reads=[B_uf], writes=[B_ui])
                    sc.op("dve", lambda h: h.tensor_copy(out=tmpf[:], in_=ui[:]), reads=[B_ui], writes=[B_tmpf])
                    sc.op("dve", lambda h: h.tensor_tensor(out=uf[:], in0=uf[:], in1=tmpf[:], op=ALU.subtract),
                          reads=[B_uf, B_tmpf], writes=[B_uf])
                    sc.op("dve", lambda h: h.tensor_single_scalar(out=tmpf[:], in_=uf[:], scalar=0.5, op=ALU.is_gt),
                          reads=[B_uf], writes=[B_tmpf])
                    sc.op("dve", lambda h: h.tensor_tensor(out=uf[:], in0=uf[:], in1=tmpf[:], op=ALU.subtract),
                          reads=[B_uf, B_tmpf], writes=[B_uf])
                    sc.op("dve", lambda h: h.tensor_single_scalar(out=tmpf[:], in_=uf[:], scalar=-0.5, op=ALU.is_lt),
                          reads=[B_uf], writes=[B_tmpf])
                    sc.op("dve", lambda h: h.tensor_tensor(out=uf[:], in0=uf[:], in1=tmpf[:], op=ALU.add),
                          reads=[B_uf, B_tmpf], writes=[B_uf])
                    sc.op("dve", lambda h: h.tensor_scalar(out=uf[:], in0=uf[:], scalar1=0.49999, scalar2=-0.49999,
                                                           op0=ALU.min, op1=ALU.max), reads=[B_uf], writes=[B_uf])
                    sc.op("act", lambda h: h.activation(out=dst, in_=uf[:], func=AF.Sin, scale=2.0 * math.pi),
                          reads=[B_uf], writes=[B_dst])

                def load_wq(ch):
                    w_d, col0 = ch["w"], ch["col0"]
                    sl = ch["wslot"]
                    sc.dma("pool", "wq%d" % sl, wq[sl][:], w_d[:, :, col0:col0 + 128], writes=[B_wq[sl]])

                def qk_units(chunks, t0):
                    units = []
                    for ci, ch in enumerate(chunks):
                        ch["wslot"] = cnt["wq"] % NWQ
                        cnt["wq"] += 1
                        ch["qs"] = cnt["qs"] % 2
                        cnt["qs"] += 1
                    for ci, ch in enumerate(chunks):
                        for st_ in range(4):
                            un = cnt["u"]
                            cnt["u"] += 1
                            bi_, x2 = un % NBIG, un % 2
                            acc = big[bi_]
                            sl, qs, d, rope, gaincol = ch["wslot"], ch["qs"], ch["d"], ch["rope"], ch["gain"]
                            cs, ce = st_ * 512, (st_ + 1) * 512

                            def s0(ci=ci, st_=st_, acc=acc, bi_=bi_, sl=sl):
                                if st_ == 0 and ci == 0:
                                    load_wq(chunks[0])
                                if st_ == 0 and ci + 1 < len(chunks):
                                    load_wq(chunks[ci + 1])
                                for k in range(8):
                                    mm(acc[:], wq[sl][:, k, :], hn[:, k, st_ * 512:(st_ + 1) * 512], k == 0, k == 7,
                                       reads=[B_wq[sl], B_hn[st_]], writes=[B_big[bi_]], sig=(k == 7))

                            def s1(acc=acc, bi_=bi_, x2=x2):
                                sc.op("act", lambda h: h.activation(out=sqs[x2][:], in_=acc[:], func=AF.Square),
                                      reads=[B_big[bi_]], writes=[B_sqs[x2]])
                                mm(ss2[x2][:], bd_ones, sqs[x2][:], True, True, reads=[B_sqs[x2], B_cmat],
                                   writes=[B_ss2[x2]], sig=True)

                            def s2a(x2=x2):
                                sc.op("act", lambda h: h.activation(out=lt2[x2][:], in_=ss2[x2][:], func=AF.Ln,
                                                                    scale=1.0 / 64, bias=EPS),
                                      reads=[B_ss2[x2]], writes=[B_lt2[x2]])
                                sc.op("act", lambda h: h.activation(out=rs2[x2][:], in_=lt2[x2][:], func=AF.Exp, scale=-0.5),
                                      reads=[B_lt2[x2]], writes=[B_rs2[x2]])

                            def s2(acc=acc, bi_=bi_, x2=x2, rope=rope, gaincol=gaincol, qs=qs, cs=cs, ce=ce):
                                if not rope:
                                    sc.op("dve", lambda h: h.scalar_tensor_tensor(
                                        out=qst[qs][:, cs:ce], in0=acc[:], scalar=cst[:, gaincol:gaincol + 1], in1=rs2[x2][:],
                                        op0=ALU.mult, op1=ALU.mult),
                                        reads=[B_big[bi_], B_rs2[x2], B_cst], writes=[B_qst[qs]])
                                    return
                                sc.op("dve", lambda h: h.scalar_tensor_tensor(
                                    out=qn[x2][:], in0=acc[:], scalar=cst[:, gaincol:gaincol + 1], in1=rs2[x2][:],
                                    op0=ALU.mult, op1=ALU.mult),
                                    reads=[B_big[bi_], B_rs2[x2], B_cst], writes=[B_qn[x2]])
                                mm(sw[x2][:], pm, qn[x2][:], True, True, reads=[B_qn[x2], B_cmat], writes=[B_sw[x2]], sig=True)

                            def s3(x2=x2, qs=qs, d=d, st_=st_, cs=cs, ce=ce, ch=ch, t0=t0, rope=rope):
                                if rope:
                                    sc.op("pool", lambda h: h.tensor_tensor(out=t1[x2][:], in0=qn[x2][:], in1=ctab[:, cs:ce],
                                                                            op=ALU.mult),
                                          reads=[B_qn[x2], B_ct], writes=[B_t1[x2]])
                                    sc.op("dve", lambda h: h.tensor_tensor(out=t2[x2][:], in0=sw[x2][:], in1=stab[:, cs:ce],
                                                                           op=ALU.mult),
                                          reads=[B_sw[x2], B_st], writes=[B_t2[x2]])
                                    m_per = 512 // d
                                    o_ap = qst[qs][:, :].rearrange("p (r m) -> p r m", r=d)[:, :, st_ * m_per:(st_ + 1) * m_per]
                                    i1 = t1[x2][:, :].rearrange("p (m r) -> p r m", r=d)
                                    i2 = t2[x2][:, :].rearrange("p (m r) -> p r m", r=d)
                                    sc.op("pool", lambda h: h.tensor_tensor(out=o_ap, in0=i1, in1=i2, op=ALU.add),
                                          reads=[B_t1[x2], B_t2[x2]], writes=[B_qst[qs]])
                                if st_ == 3:
                                    dv = ch["dest"].rearrange("p (r m) -> p r m", r=d)[:, :, t0 // d:(t0 + 2048) // d]
                                    sc.dma("sp", "qst%d" % qs, dv, qst[qs][:, :].rearrange("p (r m) -> p r m", r=d),
                                           reads=[B_qst[qs]], writes=[ch["B_dest"]])
                            units.append([s0, s1, s2a, s2, s3])
                    return units

                def v_units(w_d, col0, d, dest, Bdict, bkey, t0):
                    units = []
                    nb = 32 // d
                    nml = 2048 // (128 * d)
                    for vh in range(2):
                        sl = cnt["wv"] % 2
                        cnt["wv"] += 1
                        first = [True]
                        for r in range(d):
                            for ml in range(nml):
                                bi = r * nb + t0 // (128 * d) + ml
                                un = cnt["u"]
                                cnt["u"] += 1
                                bi_ = un % NBIG
                                st0 = 128 * ml * d + r
                                vs = cnt["vs"] % 2
                                cnt["vs"] += 1
                                ld = (r == 0 and ml == 0)

                                def s0(sl=sl, vh=vh, bi_=bi_, st0=st0, ld=ld):
                                    if ld:
                                        sc.dma("pool", "wv%d" % sl, wv[sl][:], w_d[:, :, col0 + vh * 512:col0 + (vh + 1) * 512],
                                               writes=[B_wv[sl]])
                                    for k in range(8):
                                        lhs = hn[:, k, st0:st0 + 127 * d + 1:d] if d > 1 else hn[:, k, st0:st0 + 128]
                                        mm(big[bi_][:], lhs, wv[sl][:, k, :], k == 0, k == 7,
                                           reads=[B_wv[sl]] + B_hn, writes=[B_big[bi_]], sig=(k == 7))

                                def s1(bi_=bi_, vs=vs, bi=bi, vh=vh):
                                    if vs == 0:
                                        sc.op("act", lambda h: h.activation(out=vst[vs][:], in_=big[bi_][:], func=AF.Copy),
                                              reads=[B_big[bi_]], writes=[B_vst[vs]])
                                    else:
                                        sc.op("dve", lambda h: h.tensor_copy(out=vst[vs][:], in_=big[bi_][:]),
                                              reads=[B_big[bi_]], writes=[B_vst[vs]])
                                    sc.dma("sp", "vst%d" % vs, dest[bi][:, vh * 512:(vh + 1) * 512], vst[vs][:],
                                           reads=[B_vst[vs]], writes=[gb(Bdict, (bkey, bi, vh))])
                                units.append([s0, s1])
                    return units

                for b in range(NSEQ):
                    for half in range(2):
                        t0 = half * 2048
                        for st_ in range(4):
                            load_h(ns, st_ % 2, src, b, half * 4 + st_)
                            norm_sub(ns, st_ % 2, gmcol, hn[:, :, st_ * 512:(st_ + 1) * 512], B_hn[st_])
                        units = []
                        if kind == "A":
                            for pc_ in range(4):
                                sc.dma("sp", "pi", pi[:], posb[b][:, t0 + pc_ * 512:t0 + (pc_ + 1) * 512], writes=[B_pi])
                                sc.op("dve", lambda h: h.tensor_copy(out=u[:], in_=pi[:]), reads=[B_pi], writes=[B_u])
                                sc.op("dve", lambda h: h.tensor_scalar(out=u[:], in0=u[:], scalar1=cst[:, C_INVF:C_INVF + 1],
                                                                       scalar2=None, op0=ALU.mult),
                                      reads=[B_u, B_cst], writes=[B_u])
                                trig(stab[:, pc_ * 512:(pc_ + 1) * 512], B_st, 0.0)
                                trig(ctab[:, pc_ * 512:(pc_ + 1) * 512], B_ct, 0.25)
                            aqkv = a_qkv.rearrange("(k p) n -> p k n", p=128)
                            for g in range(3):
                                chunks = []
                                for c in range(2):
                                    for pair in range(8):
                                        chunks.append(dict(w=aqkv, col0=g * 3072 + c * 1024 + pair * 128,
                                                           gain=(C_AQ if c == 0 else C_AK) + g, rope=True, d=DIL[g],
                                                           dest=qk_scr[b, g, c, pair], B_dest=gb(B_qk, (b, g, c, pair))))
                                units += qk_units(chunks, t0)
                                units += v_units(aqkv, g * 3072 + 2048, DIL[g], v_scr[b, g], B_v, (b, g), t0)
                            run_pipeline(units)
                        elif kind == "KV":
                            kvw = kv_w.rearrange("(k p) n -> p k n", p=128)
                            chunks = [dict(w=kvw, col0=pair * 128, gain=C_KVK, rope=False, d=1, dest=kB_scr[b, pair],
                                           B_dest=gb(B_qk, ("kB", b, pair))) for pair in range(8)]
                            units += qk_units(chunks, t0)
                            units += v_units(kvw, 1024, 1, vB_scr[b], B_v, ("B", b), t0)
                            for st_ in range(4):
                                un = cnt["u"]
                                cnt["u"] += 1
                                bi_, x2 = un % NBIG, un % 2

                                def s0(bi_=bi_, st_=st_):
                                    for k in range(8):
                                        mm(big[bi_][0:16, :], wf[:, k, :], hn[:, k, st_ * 512:(st_ + 1) * 512], k == 0, k == 7,
                                           reads=[B_wf, B_hn[st_]], writes=[B_big[bi_]], sig=(k == 7))

                                def s1(bi_=bi_, x2=x2, st_=st_):
                                    sc.op("act", lambda h: h.activation(out=ef[x2][:], in_=big[bi_][0:16, :], func=AF.Exp,
                                                                        scale=-1.0, bias=nbf[0:16, :]),
                                          reads=[B_big[bi_], B_nbf], writes=[B_ef[x2]])
                                    sc.op("act", lambda h: h.activation(out=lst[:, st_ * 512:(st_ + 1) * 512], in_=ef[x2][:],
                                                                        func=AF.Ln, scale=1.0, bias=1.0),
                                          reads=[B_ef[x2]], writes=[B_lst])
                                units.append([s0, s1])
                            run_pipeline(units)
                            sc.dma("sp", "lst", lf_scr[b][:, t0:t0 + 2048], lst[:], reads=[B_lst],
                                   writes=[gb(B_misc, ("lf", b, half))])
                        else:
                            bwq = b_wq.rearrange("(k p) n -> p k n", p=128)
                            chunks = [dict(w=bwq, col0=pair * 128, gain=C_BQ, rope=False, d=1, dest=qB_scr[b, pair],
                                           B_dest=gb(B_qk, ("qB", b, pair))) for pair in range(8)]
                            units += qk_units(chunks, t0)
                            run_pipeline(units)
                sc.barrier()

        def phase_cum():
            with contextlib.ExitStack() as pes:
                lin = sb(pes, [16, S], F32, "lin")
                zer = sb(pes, [16, S], F32, "zer")
                lc = sb(pes, [16, S], F32, "lc")
                r1 = sb(pes, [16, S], F32, "r1")
                tf = sb(pes, [16, S], F32, "tf")
                pos3 = [sb(pes, [16, S], BF16, "p3") for _ in range(3)]
                neg3 = [sb(pes, [16, S], BF16, "n3") for _ in range(3)]
                one = sb(pes, [16, S], BF16, "one")
                B_lin, B_z, B_lc, B_r1, B_tf, B_one = Buf(), Buf(), Buf(), Buf(), Buf(), Buf()
                B_p3 = [Buf() for _ in range(3)]
                B_n3 = [Buf() for _ in range(3)]
                sc.op("pool", lambda h: h.memset(zer[:], 0.0), writes=[B_z])
                sc.op("pool", lambda h: h.memset(one[:], 1.0), writes=[B_one])
                for b in range(NSEQ):
                    sc.dma("sp", "lin", lin[:], lf_scr[b], reads=[gb(B_misc, ("lf", b, 0)), gb(B_misc, ("lf", b, 1))],
                           writes=[B_lin])
                    sc.op("dve", lambda h: h.tensor_scalar(out=lin[:], in0=lin[:], scalar1=8.0, scalar2=None, op0=ALU.mult),
                          reads=[B_lin], writes=[B_lin])
                    sc.op("dve", lambda h: h.tensor_tensor_scan(out=lc[:], data0=lin[:], data1=zer[:], initial=0.0,
                                                                op0=ALU.add, op1=ALU.add),
                          reads=[B_lin, B_z], writes=[B_lc])
                    cur, Bcur = lc, B_lc
                    for i in range(3):
                        sc.op("dve", lambda h, i=i, cur=cur: h.tensor_copy(out=pos3[i][:], in_=cur[:]),
                              reads=[Bcur], writes=[B_p3[i]])
                        sc.op("dve", lambda h, i=i: h.tensor_scalar(out=neg3[i][:], in0=pos3[i][:], scalar1=-1.0,
                                                                    scalar2=None, op0=ALU.mult),
                              reads=[B_p3[i]], writes=[B_n3[i]])
                        if i < 2:
                            sc.op("dve", lambda h, i=i: h.tensor_copy(out=tf[:], in_=pos3[i][:]),
                                  reads=[B_p3[i]], writes=[B_tf])
                            sc.op("dve", lambda h, cur=cur: h.tensor_tensor(out=r1[:], in0=cur[:], in1=tf[:], op=ALU.subtract),
                                  reads=[Bcur, B_tf], writes=[B_r1])
                            cur, Bcur = r1, B_r1
                    Bq = gb(B_misc, ("qb", b))
                    Bk = gb(B_misc, ("kb", b))
                    for i in range(3):
                        sc.dma("sp", "cq%d" % i, qb_scr[b][:, i, :], neg3[i][:], reads=[B_n3[i]], writes=[gb(B_misc, ("qb", b, i))])
                        sc.dma("sp", "ck%d" % i, kb_scr[b][:, 3 + i, :], pos3[i][:], reads=[B_p3[i]], writes=[gb(B_misc, ("kb", b, i))])
                        sc.dma("sp", "co%d" % i, qb_scr[b][:, 3 + i, :], one[:], reads=[B_one], writes=[gb(B_misc, ("qb1", b, i))])
                        sc.dma("sp", "cp%d" % i, kb_scr[b][:, i, :], one[:], reads=[B_one], writes=[gb(B_misc, ("kb1", b, i))])
                sc.barrier()

        def phase_attn(kind):
            with contextlib.ExitStack() as pes:
                if kind == "A":
                    acc = [sb(pes, [128, S], F32, "acc") for _ in range(2)]
                    B_acc = [Buf(), Buf()]
                NQS = 2 if kind == "A" else 1
                qH = [[sb(pes, [128, S], BF16, "qH") for _ in range(2)] for _ in range(NQS)]
                B_q = [[Buf(), Buf()] for _ in range(NQS)]
                qS = sb(pes, [128, S], BF16, "qS")
                B_qS = Buf()
                if kind == "A":
                    kT = [sb(pes, [128, S], BF16, "kT") for _ in range(2)]
                    B_k1 = [Buf(), Buf()]
                else:
                    kH = [[sb(pes, [128, S], BF16, "kH") for _ in range(2)]]
                    B_k = [[Buf(), Buf()]]
                    kS = sb(pes, [128, S], BF16, "kS")
                    B_kS = Buf()
                    bst = sb(pes, [6, S], BF16, "bst")
                    B_bst = Buf()
                V2 = [[sb(pes, [128, 32, 128], BF16, "V") for _ in range(2)] for _ in range(2)]
                B_V = [[Buf(), Buf()], [Buf(), Buf()]]
                B_V1 = Buf()
                op_ = [sb(pes, [128, S], BF16, "opair") for _ in range(2)]
                B_opc = [[Buf() for _ in range(8)] for _ in range(2)]
                opc = [0]
                if kind == "A":
                    dd = sb(pes, [128, S], F32, "dd")
                    B_ddc = [Buf() for _ in range(8)]
                else:
                    rdt = [sb(pes, [128, 512], F32, "rdt") for _ in range(2)]
                    B_rdt = [Buf(), Buf()]
                NP = 4
                pt = [sb(pes, [128, 512], BF16, "pt") for _ in range(NP)]
                B_pt = [Buf() for _ in range(NP)]
                stp = [ps(pes, "st") for _ in range(NP)]
                B_stp = [Buf() for _ in range(NP)]
                accp = [ps(pes, "ap") for _ in range(2)]
                B_accp = [Buf(), Buf()]
                for sl in range(2):
                    sc.op("pool", lambda h, sl=sl: h.memset(V2[sl][0][:, :, 64:128], 1.0), writes=[B_V1])
                    sc.op("pool", lambda h, sl=sl: h.memset(V2[sl][1][:, :, 0:64], 1.0), writes=[B_V1])
                cnt = {"ld": 0, "st": 0, "ap": 0, "hb": 0}

                def finalize(b, pair):
                    oi = opc[0] % 2
                    opc[0] += 1
                    for c in range(8):
                        cs = slice(c * 512, (c + 1) * 512)
                        sc.op("dve", lambda h, cs=cs: h.tensor_copy(out=dd[0:64, cs], in_=acc[0][64:128, cs]),
                              reads=[B_acc[0]], writes=[B_ddc[c]])
                        sc.op("dve", lambda h, cs=cs: h.tensor_copy(out=dd[64:128, cs], in_=acc[1][0:64, cs]),
                              reads=[B_acc[1]], writes=[B_ddc[c]])
                        sc.op("act", lambda h, cs=cs: h.activation(out=dd[:, cs], in_=dd[:, cs], func=AF.Ln),
                              reads=[B_ddc[c]], writes=[B_ddc[c]])
                        sc.op("act", lambda h, cs=cs: h.activation(out=dd[:, cs], in_=dd[:, cs], func=AF.Exp, scale=-1.0),
                              reads=[B_ddc[c]], writes=[B_ddc[c]])
                    for c in range(8):
                        cs = slice(c * 512, (c + 1) * 512)
                        sc.op("pool", lambda h, cs=cs, oi=oi: h.tensor_tensor(out=op_[oi][0:64, cs], in0=acc[0][0:64, cs],
                                                                              in1=dd[0:64, cs], op=ALU.mult),
                              reads=[B_acc[0], B_ddc[c]], writes=[B_opc[oi][c]])
                        sc.op("dve", lambda h, cs=cs, oi=oi: h.tensor_tensor(out=op_[oi][64:128, cs], in0=acc[1][64:128, cs],
                                                                             in1=dd[64:128, cs], op=ALU.mult),
                              reads=[B_acc[1], B_ddc[c]], writes=[B_opc[oi][c]])
                    sc.dma("sp", "opair%d" % oi, o_scr[b][pair * 128:(pair + 1) * 128, :], op_[oi][:], reads=B_opc[oi],
                           writes=[B_o[b][pair]])

                def mask_copy(e, dst, src_, col, reads, writes):
                    sc.op(e, lambda h: h.tensor_scalar(out=dst[:], in0=src_[:], scalar1=cst[:, col:col + 1], scalar2=None,
                                                       op0=ALU.mult), reads=reads + [B_cst], writes=writes)

                def prepare(j):
                    b, pair, g = jobs[j]
                    qs_ = j % NQS
                    mask_copy("dve", qH[qs_][0], qS, C_MA, [B_qS], [B_q[qs_][0]])
                    mask_copy("dve", qH[qs_][1], qS, C_MB, [B_qS], [B_q[qs_][1]])
                    if kind == "B":
                        mask_copy("dve", kH[0][0], kS, C_MA, [B_kS], [B_k[0][0]])
                        mask_copy("dve", kH[0][1], kS, C_MB, [B_kS], [B_k[0][1]])
                        for hd in range(2):
                            hh = pair * 2 + hd
                            r0 = 64 if hd == 0 else 0
                            rdq = [gb(B_misc, ("qb", b, i)) for i in range(3)] + [gb(B_misc, ("qb1", b, i)) for i in range(3)]
                            rdk = [gb(B_misc, ("kb", b, i)) for i in range(3)] + [gb(B_misc, ("kb1", b, i)) for i in range(3)]
                            for (srcd, rd_, dst, Bd) in ((qb_scr[b, hh], rdq, qH[0][hd], B_q[0][hd]),
                                                         (kb_scr[b, hh], rdk, kH[0][hd], B_k[0][hd])):
                                sc.dma("sp", "bst", bst[:], srcd, reads=rd_, writes=[B_bst])
                                sc.op("dve", lambda h, dst=dst, r0=r0: h.tensor_copy(out=dst[r0:r0 + 6, :], in_=bst[:]),
                                      reads=[B_bst], writes=[Bd])

                def load_group(sl, b, pair, qsrc, ksrc, vsrc, Bq_d, Bk_d, vkey):
                    sc.dma("sp", "qS", qS[:], qsrc, reads=[Bq_d], writes=[B_qS])
                    if kind == "A":
                        sc.dma("sp", "kT%d" % sl, kT[sl][:], ksrc, reads=[Bk_d], writes=[B_k1[sl]])
                    else:
                        sc.dma("sp", "kS", kS[:], ksrc, reads=[Bk_d], writes=[B_kS])
                    vr = vsrc.rearrange("bi i c -> i bi c")
                    rd = [gb(B_v, (vkey, bi, pair // 4)) for bi in range(32)]
                    sc.dma("sp", "VA%d" % sl, V2[sl][0][:, :, 0:64], vr[:, :, pair * 128:pair * 128 + 64],
                           reads=rd, writes=[B_V[sl][0]])
                    sc.dma("sp", "VB%d" % sl, V2[sl][1][:, :, 64:128], vr[:, :, pair * 128 + 64:pair * 128 + 128],
                           reads=rd, writes=[B_V[sl][1]])
                    return sl

                def exp_stage(si, n):
                    def f():
                        sc.op("act", lambda h: h.activation(out=pt[si][:, 0:n], in_=stp[si][:, 0:n], func=AF.Exp,
                                                            scale=0.125),
                              reads=[B_stp[si]], writes=[B_pt[si]])
                    return f

                if kind == "A":
                    jobs = [(b, pair, g) for b in range(NSEQ) for pair in range(8) for g in range(3)]
                else:
                    jobs = [(b, pair, 0) for b in range(NSEQ) for pair in range(8)]

                def issue_load(j):
                    if j >= len(jobs):
                        return
                    b, pair, g = jobs[j]
                    if kind == "A":
                        load_group(j % 2, b, pair, qk_scr[b, g, 0, pair], qk_scr[b, g, 1, pair], v_scr[b, g],
                                   gb(B_qk, (b, g, 0, pair)), gb(B_qk, (b, g, 1, pair)), (b, g))
                        prepare(j)
                    else:
                        load_group(j % 2, b, pair, qB_scr[b, pair], kB_scr[b, pair], vB_scr[b],
                                   gb(B_qk, ("qB", b, pair)), gb(B_qk, ("kB", b, pair)), ("B", b))

                conv = []
                if kind == "A":
                    cv = [sb(pes, [128, 5632], BF16, "cv") for _ in range(2)]
                    B_cv = [[Buf(), Buf()], [Buf(), Buf()]]
                    for (l_, i_) in ((0, 1), (1, 0), (1, 1)):
                        conv += [(l_, i_, "in", pc_) for pc_ in range(11)] + [(l_, i_, "out", n_) for n_ in range(4)]
                cvn = [0]

                def conv_extras(steps):
                    ex = []
                    for (s_ld, s_st) in steps:
                        if cvn[0] >= len(conv):
                            break
                        l_, i_, kd, idx = conv[cvn[0]]
                        sl_ = cvn[0] % 2
                        cvn[0] += 1
                        key = (l_, i_, kd, idx)
                        B_wbf[key] = Buf()
                        if kd == "in":
                            wd = w_in[l_, i_].rearrange("(k p) n -> p k n", p=128)
                            v3 = cv[sl_][:, 0:4096].rearrange("p (k n) -> p k n", k=8)

                            def ld(wd=wd, v3=v3, idx=idx, sl_=sl_):
                                sc.dma("pool", "cv%da" % sl_, v3[:, :, 0:256], wd[:, :, idx * 256:(idx + 1) * 256],
                                       writes=[B_cv[sl_][0]])
                                sc.dma("pool", "cv%db" % sl_, v3[:, :, 256:512],
                                       wd[:, :, DFF + idx * 256:DFF + (idx + 1) * 256], writes=[B_cv[sl_][1]])

                            def st(l_=l_, i_=i_, idx=idx, sl_=sl_, key=key):
                                sc.dma("sp", "cvs%d" % sl_, winbf[l_, i_, idx], cv[sl_][:, 0:4096],
                                       reads=B_cv[sl_], writes=[B_wbf[key]])
                        else:
                            wd = w_out[l_, i_].rearrange("(j p) n -> p j n", p=128)
                            v3 = cv[sl_][:, :].rearrange("p (j n) -> p j n", j=22)

                            def ld(wd=wd, v3=v3, idx=idx, sl_=sl_):
                                sc.dma("pool", "cv%da" % sl_, v3, wd[:, :, idx * 256:(idx + 1) * 256],
                                       writes=[B_cv[sl_][0], B_cv[sl_][1]])

                            def st(l_=l_, i_=i_, idx=idx, sl_=sl_, key=key):
                                sc.dma("sp", "cvs%d" % sl_, woutbf[l_, i_, idx], cv[sl_][:, :],
                                       reads=B_cv[sl_], writes=[B_wbf[key]])
                        ex.append((s_ld, ld))
                        ex.append((s_st, st))
                    return ex

                issue_load(0)
                jobno = [0]
                for b in range(NSEQ):
                    for pair in range(8):
                        units = []
                        if kind == "A":
                            for hd in range(2):
                                sc.op("pool", lambda h, hd=hd: h.memset(acc[hd][:], 0.0), writes=[B_acc[hd]])
                            for g in range(3):
                                d = DIL[g]
                                nb = 32 // d
                                csz = min(4, nb)
                                jn = jobno[0]
                                jobno[0] += 1
                                sl = jn % 2
                                ucount = [0]
                                for hd in range(2):
                                    rb = hd * 64
                                    V = V2[sl][hd]
                                    for r in range(d):
                                        chunk_base = cnt["ap"]
                                        cnt["ap"] += nb // csz
                                        for m in range(nb):
                                            bi = r * nb + m
                                            base = bi * 128
                                            n = 256 if m + 1 < nb else 128
                                            si = cnt["st"] % NP
                                            cnt["st"] += 1

                                            pre = jn + 1 if ucount[0] == 6 else None
                                            ucount[0] += 1

                                            def sA(si=si, n=n, sl=sl, hd=hd, base=base, pre=pre):
                                                if pre is not None:
                                                    issue_load(pre)
                                                mm(stp[si][:, 0:n], kT[sl][:, base:base + 128],
                                                   qH[sl][hd][:, base:base + n], True, False,
                                                   reads=[B_k1[sl], B_q[sl][hd]], writes=[B_stp[si]], sig=False)
                                                mm(stp[si][:, 0:n], ident, band[:, 0:n], False, True,
                                                   reads=[B_cmat], writes=[B_stp[si]], sig=True)

                                            def sP(si=si, n=n, sl=sl, hd=hd, V=V, bi=bi, m=m, r=r, d=d, csz=csz,
                                                   chunk_base=chunk_base):
                                                ai = (chunk_base + m // csz) % 2
                                                col = (m % csz) * 128
                                                last_in_chunk = (m % csz == csz - 1)
                                                mm(accp[ai][:, col:col + 128], V[:, bi, :], pt[si][:, 0:128], m == 0, True,
                                                   reads=[B_pt[si], B_V[sl][hd], B_V1], writes=[B_accp[ai]],
                                                   sig=last_in_chunk)
                                                if last_in_chunk:
                                                    c0 = (m - csz + 1) * 128 * d + r
                                                    w = csz * 128
                                                    a_ap = acc[hd][:, c0:c0 + (w - 1) * d + 1:d] if d > 1 else acc[hd][:, c0:c0 + w]
                                                    sc.op("dve", lambda h: h.tensor_tensor(
                                                        out=a_ap, in0=accp[ai][:, 0:w], in1=a_ap, op=ALU.add),
                                                        reads=[B_accp[ai], B_acc[hd]], writes=[B_acc[hd]])
                                                if n == 256:
                                                    ai2 = (chunk_base + (m + 1) // csz) % 2
                                                    col2 = ((m + 1) % csz) * 128
                                                    mm(accp[ai2][:, col2:col2 + 128], V[:, bi, :], pt[si][:, 128:256], True, False,
                                                       reads=[B_pt[si], B_V[sl][hd], B_V1], writes=[B_accp[ai2]], sig=True)
                                            units.append([sA, exp_stage(si, n), sP])
                        else:
                            jn = jobno[0]
                            jobno[0] += 1
                            sl = jn % 2
                            ucount = [0]
                            prepare(jn)
                            oi = opc[0] % 2
                            opc[0] += 1
                            for hd in range(2):
                                hh = pair * 2 + hd
                                rb = hd * 64
                                V = V2[sl][hd]
                                for qt in range(8):
                                    ai = cnt["ap"] % 2
                                    cnt["ap"] += 1
                                    nkb = 4 * qt + 4
                                    for kb_ in range(nkb):
                                        j = kb_ - 4 * qt
                                        c0 = 128 * max(j, 0)
                                        n = 512 - c0
                                        q0 = qt * 512 + c0
                                        si = cnt["st"] % NP
                                        cnt["st"] += 1

                                        pre = jn + 1 if ucount[0] == 6 else None
                                        ucount[0] += 1

                                        def sA(si=si, n=n, sl=sl, hd=hd, kb_=kb_, q0=q0, j=j, pre=pre):
                                            if pre is not None:
                                                issue_load(pre)
                                            rds = [B_k[0][hd], B_q[0][hd]]
                                            mm(stp[si][:, 0:n], kH[0][hd][:, kb_ * 128:(kb_ + 1) * 128],
                                               qH[0][hd][:, q0:q0 + n], True, j < 0,
                                               reads=rds, writes=[B_stp[si]], sig=(j < 0))
                                            if j >= 0:
                                                mm(stp[si][:, 0:128], ident, caus, False, True,
                                                   reads=[B_cmat], writes=[B_stp[si]], sig=True)

                                        def sP(si=si, n=n, sl=sl, hd=hd, V=V, kb_=kb_, c0=c0, ai=ai, nkb=nkb, qt=qt, oi=oi):
                                            mm(accp[ai][:, c0:512], V[:, kb_, :], pt[si][:, 0:n], kb_ == 0, kb_ == nkb - 1,
                                               reads=[B_pt[si], B_V[sl][hd], B_V1], writes=[B_accp[ai]], sig=(kb_ == nkb - 1))
                                            if kb_ == nkb - 1:
                                                nr = slice(0, 64) if hd == 0 else slice(64, 128)
                                                dr = slice(64, 128) if hd == 0 else slice(0, 64)
                                                ri = (qt + hd) % 2
                                                sc.op("dve", lambda h: h.reciprocal(out=rdt[ri][nr, :], in_=accp[ai][dr, :]),
                                                      reads=[B_accp[ai]], writes=[B_rdt[ri]])
                                                sc.op("dve", lambda h: h.tensor_tensor(
                                                    out=op_[oi][nr, qt * 512:(qt + 1) * 512], in0=accp[ai][nr, :],
                                                    in1=rdt[ri][nr, :], op=ALU.mult),
                                                    reads=[B_accp[ai], B_rdt[ri]], writes=[B_opc[oi][qt]])
                                        units.append([sA, exp_stage(si, n), sP])
                        run_pipeline(units, conv_extras([(20, 50), (80, 110), (140, 170)]) if kind == "A" else None)
                        if kind == "A":
                            finalize(b, pair)
                        else:
                            sc.dma("sp", "opair%d" % oi, o_scr[b][pair * 128:(pair + 1) * 128, :], op_[oi][:], reads=B_opc[oi],
                                   writes=[B_o[b][pair]])
                sc.barrier()

        def phase_wo(w_d):
            src = state["src"]
            with contextlib.ExitStack() as pes:
                wo = sb(pes, [128, 8, D], BF16, "wo")
                B_wo = Buf()
                sc.dma("pool", "wo", wo[:], w_d.rearrange("(k p) n -> p k n", p=128), writes=[B_wo])
                hs = [sb(pes, [128, 8, 512], F32, "hsw") for _ in range(2)]
                B_hs = [Buf(), Buf()]
                ot = [sb(pes, [128, 8, 512], BF16, "ot") for _ in range(2)]
                B_ot = [Buf(), Buf()]
                pp = [ps(pes, "wo") for _ in range(2)]
                B_pp = [Buf(), Buf()]
                c = 0
                for b in range(NSEQ):
                    for t in range(8):
                        i = (b * 8 + t) % 2
                        sc.dma("sp", "hsw%d" % i, hs[i][:], src[b][:, t * 512:(t + 1) * 512].rearrange("(k p) t -> p k t", p=128),
                               reads=[B_y[b][t]], writes=[B_hs[i]])
                        sc.dma("sp", "ot%d" % i, ot[i][:], o_scr[b][:, t * 512:(t + 1) * 512].rearrange("(k p) t -> p k t", p=128),
                               reads=B_o[b], writes=[B_ot[i]])
                        for n in range(8):
                            pi_ = c % 2
                            c += 1
                            for k in range(8):
                                mm(pp[pi_][:], wo[:, k, n * 128:(n + 1) * 128], ot[i][:, k, :], k == 0, k == 7,
                                   reads=[B_wo, B_ot[i]], writes=[B_pp[pi_]], sig=(k == 7))
                            sc.op("dve", lambda h, pi_=pi_, i=i, n=n: h.tensor_tensor(
                                out=hs[i][:, n, :], in0=pp[pi_][:], in1=hs[i][:, n, :], op=ALU.add),
                                reads=[B_pp[pi_], B_hs[i]], writes=[B_hs[i]])
                        sc.dma("sp", "hsw%d" % i, yT[b][:, t * 512:(t + 1) * 512].rearrange("(k p) t -> p k t", p=128), hs[i][:],
                               reads=[B_hs[i]], writes=[B_y[b][t]])
                sc.barrier()
            state["src"] = yT

        phases = [
            lambda: phase_ffn(0, 0),
            lambda: phase_proj("A"),
            lambda: phase_attn("A"),
            lambda: phase_wo(a_wo),
            lambda: phase_ffn(0, 1),
            lambda: phase_proj("KV"),
            lambda: phase_cum(),
            lambda: phase_ffn(1, 0),
            lambda: phase_proj("QB"),
            lambda: phase_attn("B"),
            lambda: phase_wo(b_wo),
            lambda: phase_ffn(1, 1),
        ]
        for i, p in enumerate(phases):
            if i >= stop_after:
                break
            p()
        sc.barrier()
        with nc.Block() as block:
            sc.emit(block)
    return nc


def host_consts(inp):
    cst = np.zeros((128, NCST), np.float32)
    fn = np.asarray(inp["ffn_norm"], np.float32)
    for l in range(2):
        for i in range(2):
            cst[:, C_GF + (l * 2 + i) * 8:C_GF + (l * 2 + i) * 8 + 8] = fn[l, i].reshape(8, 128).T
    mn = np.asarray(inp["mix_norm"], np.float32)
    for l in range(2):
        cst[:, C_GM + l * 8:C_GM + l * 8 + 8] = mn[l].reshape(8, 128).T
    cst[:, C_GKV:C_GKV + 8] = np.asarray(inp["kv_norm"], np.float32).reshape(8, 128).T
    for g in range(3):
        cst[:, C_AQ + g] = np.tile(np.asarray(inp["a_q_norm"], np.float32)[0, g], 2)
        cst[:, C_AK + g] = np.tile(np.asarray(inp["a_k_norm"], np.float32)[0, g], 2)
    cst[:, C_KVK] = np.tile(np.asarray(inp["kv_k_norm"], np.float32), 2)
    cst[:, C_BQ] = np.tile(np.asarray(inp["b_q_norm"], np.float32)[0], 2)
    invf = (500000.0 ** (-np.arange(0, 16, 2, dtype=np.float64) / 16)) / (2 * np.pi)
    for p in range(128):
        j = p % 64
        cst[p, C_INVF] = invf[j % 8] if j < 16 else 0.0
    cst[0:16, C_BF] = np.asarray(inp["kv_b_f"], np.float32)
    cst[0:64, C_MA] = 1.0
    cst[64:128, C_MB] = 1.0
    cm = np.zeros((128, NMAT), np.float32)
    cm[:, M_ID:M_ID + 128] = np.eye(128)
    cm[:, M_ONES:M_ONES + 128] = 1.0
    for hb in range(2):
        cm[hb * 64:(hb + 1) * 64, M_BD + hb * 64:M_BD + (hb + 1) * 64] = 1.0
    for m in range(128):
        j = m % 64
        if j < 8:
            cm[m + 8, M_PM + m] = -1.0
        elif j < 16:
            cm[m - 8, M_PM + m] = 1.0
    kj = np.arange(128)[:, None]
    qc = np.arange(256)[None, :]
    cm[:, M_BAND:M_BAND + 256] = np.where((qc - kj >= 0) & (qc - kj <= 128), 0.0, NEG)
    qi = np.arange(128)[None, :]
    cm[:, M_CAUS:M_CAUS + 128] = np.where(kj <= qi, 0.0, NEG)
    return cst, cm


_CACHE = {}


def kernel(**inp):
    x = np.asarray(inp["x"], np.float32)
    pos = np.asarray(inp["positions"], np.int32)
    cst, cm = host_consts(inp)
    if "nc" not in _CACHE:
        _CACHE["nc"] = build_program()
    nc = _CACHE["nc"]
    shared = {
        "cst": cst, "cmat": cm,
        "ffn_w_in": np.ascontiguousarray(inp["ffn_w_in"], np.float32),
        "ffn_w_out": np.ascontiguousarray(inp["ffn_w_out"], np.float32),
        "a_w_qkv": np.ascontiguousarray(np.asarray(inp["a_w_qkv"], np.float32)[0]),
        "a_w_o": np.ascontiguousarray(np.asarray(inp["a_w_o"], np.float32)[0]),
        "kv_w": np.ascontiguousarray(inp["kv_w"], np.float32),
        "b_w_q": np.ascontiguousarray(np.asarray(inp["b_w_q"], np.float32)[0]),
        "b_w_o": np.ascontiguousarray(np.asarray(inp["b_w_o"], np.float32)[0]),
    }
    in_maps = []
    for c in range(NCORES):
        xs = x[c * NSEQ:(c + 1) * NSEQ]
        m = dict(shared)
        m["xT"] = np.ascontiguousarray(xs.transpose(0, 2, 1))
        m["posb"] = np.ascontiguousarray(np.broadcast_to(pos[c * NSEQ:(c + 1) * NSEQ, None, :], (NSEQ, 128, S)))
        in_maps.append(m)
    res = run_bass_kernel_spmd(nc, in_maps, core_ids=list(range(NCORES)))
    out = np.empty((NCORES * NSEQ, S, D), np.float32)
    for c in range(NCORES):
        out[c * NSEQ:(c + 1) * NSEQ] = np.asarray(res.results[c]["yT"]).transpose(0, 2, 1)
    return out
```

```python
import contextlib
import math
import numpy as np
import concourse.bass as bass
import concourse.mybir as mybir
from concourse.bass_utils import run_bass_kernel_spmd

F32 = mybir.dt.float32
BF16 = mybir.dt.bfloat16
I32 = mybir.dt.int32
AF = mybir.ActivationFunctionType
ALU = mybir.AluOpType

S = 4096
D = 1024
DFF = 2816
NSEQ = 2
NCORES = 8
EPS = 1e-6
NEG = -30000.0
DIL = (1, 4, 16)

C_GF = 0
C_GM = 32
C_GKV = 48
C_AQ = 56
C_AK = 59
C_KVK = 62
C_BQ = 63
C_INVF = 64
C_BF = 65
C_MA = 66
C_MB = 67
NCST = 68
M_ID = 0
M_ONES = 128
M_BD = 256
M_PM = 384
M_BAND = 512
M_CAUS = 768
NMAT = 896


class Buf:
    __slots__ = ("w", "r")

    def __init__(self):
        self.w = None
        self.r = {}


class Eng:
    def __init__(self, name, sem):
        self.name = name
        self.sem = sem
        self.n = 0
        self.waited = {}
        self.ops = []
        self.pending = False


class Slot:
    def __init__(self, name, sem):
        self.name = name
        self.sem = sem
        self.n = 0


class Sched:
    SAME_ENGINE = ("act", "dve", "pool")

    def __init__(self, nc, es):
        self.nc = nc
        self.es = es
        self.eng = {}
        for name in ("pe", "act", "dve", "pool", "sp"):
            self.eng[name] = Eng(name, es.enter_context(nc.semaphore("e_" + name)))
        self.slots = {}
        self.semof = {e.name: e.sem for e in self.eng.values()}

    def slot(self, name):
        if name not in self.slots:
            s = Slot(name, self.es.enter_context(self.nc.semaphore("d_" + name)))
            self.slots[name] = s
            self.semof[name] = s.sem
        return self.slots[name]

    def _waits(self, eng, reads, writes):
        need = {}

        def add(tok):
            if tok is None:
                return
            k, v = tok
            if k == eng.name:
                if eng.name not in self.SAME_ENGINE or v > eng.n:
                    return
            if eng.waited.get(k, 0) >= v:
                return
            if need.get(k, 0) < v:
                need[k] = v

        for b in reads:
            add(b.w)
        for b in writes:
            add(b.w)
            for k, v in b.r.items():
                add((k, v))
        for k, v in need.items():
            eng.waited[k] = v
        return [(self.semof[k], v) for k, v in need.items()]

    def _commit(self, tok, reads, writes):
        k, v = tok
        for b in reads:
            if b.r.get(k, 0) < v:
                b.r[k] = v
        for b in writes:
            b.w = tok
            b.r = {}

    def op(self, e, fn, reads=(), writes=(), sig=True):
        eng = self.eng[e]
        waits = self._waits(eng, reads, writes)
        tok = (eng.name, eng.n + 1)
        eng.ops.append((waits, fn, eng.sem if sig else None, 1))
        if sig:
            eng.n += 1
            eng.pending = False
        else:
            eng.pending = True
        self._commit(tok, reads, writes)
        return tok

    def dma(self, q, slotname, out, in_, reads=(), writes=()):
        eng = self.eng[q]
        sl = self.slot(slotname)
        waits = self._waits(eng, reads, writes)
        sl.n += 1
        tok = (sl.name, 16 * sl.n)
        eng.ops.append((waits, (lambda h, o=out, i=in_: h.dma_start(out=o, in_=i)), sl.sem, 16))
        self._commit(tok, reads, writes)
        return tok

    def barrier(self):
        toks = [(e.name, e.n) for e in self.eng.values() if e.n > 0]
        toks += [(s.name, 16 * s.n) for s in self.slots.values() if s.n > 0]
        for e in self.eng.values():
            assert not e.pending, e.name
            need = []
            for k, v in toks:
                if k == e.name:
                    continue
                if e.waited.get(k, 0) >= v:
                    continue
                e.waited[k] = v
                need.append((self.semof[k], v))
            if need:
                e.ops.append((need, None, None, 0))

    def emit(self, block):
        hmap = {"pe": block.tensor, "act": block.scalar, "dve": block.vector,
                "pool": block.gpsimd, "sp": block.sync}
        for name, deco in hmap.items():
            eng = self.eng[name]
            assert not eng.pending, name

            def body(h, eng=eng):
                for waits, fn, sem, inc in eng.ops:
                    for s, v in waits:
                        h.wait_ge(s, v)
                    if fn is None:
                        continue
                    inst = fn(h)
                    if sem is not None:
                        inst.then_inc(sem, inc)
            deco(body)


def build_program(stop_after=99):
    nc = bass.Bass("TRN2", target_bir_lowering=False)
    dt = nc.dram_tensor
    xT = dt("xT", [NSEQ, D, S], F32, kind="ExternalInput").ap()
    posb = dt("posb", [NSEQ, 128, S], I32, kind="ExternalInput").ap()
    cst_d = dt("cst", [128, NCST], F32, kind="ExternalInput").ap()
    cmat_d = dt("cmat", [128, NMAT], F32, kind="ExternalInput").ap()
    w_in = dt("ffn_w_in", [2, 2, D, 2 * DFF], F32, kind="ExternalInput").ap()
    w_out = dt("ffn_w_out", [2, 2, DFF, D], F32, kind="ExternalInput").ap()
    a_qkv = dt("a_w_qkv", [D, 9216], F32, kind="ExternalInput").ap()
    a_wo = dt("a_w_o", [D, D], F32, kind="ExternalInput").ap()
    kv_w = dt("kv_w", [D, 2064], F32, kind="ExternalInput").ap()
    b_wq = dt("b_w_q", [D, D], F32, kind="ExternalInput").ap()
    b_wo = dt("b_w_o", [D, D], F32, kind="ExternalInput").ap()
    yT = dt("yT", [NSEQ, D, S], F32, kind="ExternalOutput").ap()
    qk_scr = dt("qk_scr", [NSEQ, 3, 2, 8, 128, S], BF16, kind="Internal").ap()
    v_scr = dt("v_scr", [NSEQ, 3, 32, 128, D], BF16, kind="Internal").ap()
    o_scr = dt("o_scr", [NSEQ, D, S], BF16, kind="Internal").ap()
    kB_scr = dt("kB_scr", [NSEQ, 8, 128, S], BF16, kind="Internal").ap()
    vB_scr = dt("vB_scr", [NSEQ, 32, 128, D], BF16, kind="Internal").ap()
    qB_scr = dt("qB_scr", [NSEQ, 8, 128, S], BF16, kind="Internal").ap()
    lf_scr = dt("lf_scr", [NSEQ, 16, S], F32, kind="Internal").ap()
    qb_scr = dt("qb_scr", [NSEQ, 16, 6, S], BF16, kind="Internal").ap()
    kb_scr = dt("kb_scr", [NSEQ, 16, 6, S], BF16, kind="Internal").ap()
    winbf = dt("winbf", [2, 2, 11, 128, 8 * 512], BF16, kind="Internal").ap()
    woutbf = dt("woutbf", [2, 2, 4, 128, 22 * 256], BF16, kind="Internal").ap()

    with contextlib.ExitStack() as es:
        sc = Sched(nc, es)
        uid = [0]

        def sb(es_, shape, dtype, name="t"):
            uid[0] += 1
            return es_.enter_context(nc.sbuf_tensor("%s%d" % (name, uid[0]), shape, dtype))

        def ps(es_, name="p", shape=(128, 512)):
            uid[0] += 1
            return es_.enter_context(nc.psum_tensor("%s%d" % (name, uid[0]), list(shape), F32))

        cst = sb(es, [128, NCST], F32, "cst")
        cmat = sb(es, [128, NMAT], BF16, "cmat")
        nbf = sb(es, [128, 1], F32, "nbf")
        B_cst = Buf()
        B_cmat = Buf()
        B_nbf = Buf()
        sc.dma("sp", "cst", cst[:], cst_d, writes=[B_cst])
        sc.dma("pool", "cmat", cmat[:], cmat_d, writes=[B_cmat])
        sc.op("dve", lambda h: h.tensor_scalar(out=nbf[0:16, :], in0=cst[0:16, C_BF:C_BF + 1],
                                               scalar1=-1.0, scalar2=None, op0=ALU.mult),
              reads=[B_cst], writes=[B_nbf])
        ident = cmat[:, M_ID:M_ID + 128]
        ones_bf = cmat[:, M_ONES:M_ONES + 128]
        bd_ones = cmat[:, M_BD:M_BD + 128]
        pm = cmat[:, M_PM:M_PM + 128]
        band = cmat[:, M_BAND:M_BAND + 256]
        caus = cmat[:, M_CAUS:M_CAUS + 128]

        B_y = [[Buf() for _ in range(8)] for _ in range(NSEQ)]
        B_qk = {}
        B_v = {}
        B_o = [[Buf() for _ in range(8)] for _ in range(NSEQ)]
        B_misc = {}

        def gb(dct, key):
            if key not in dct:
                dct[key] = Buf()
            return dct[key]

        B_wbf = {}
        state = {"src": xT}

        def mm(out, lhsT, rhs, start, stop, reads, writes, sig):
            sc.op("pe", lambda h: h.matmul(out, lhsT=lhsT, rhs=rhs, start=start, stop=stop,
                                           skip_group_check=True),
                  reads=reads, writes=writes, sig=sig)

        class NormScratch:
            def __init__(self, es_, ss=None, B_ss=None, nhs=2):
                self.hs = [sb(es_, [128, 8, 512], F32, "hs") for _ in range(nhs)]
                self.B_hs = [Buf() for _ in range(nhs)]
                self.sqb = sb(es_, [128, 8, 512], BF16, "sqb")
                self.B_sqb = Buf()
                self.lt = sb(es_, [128, 512], F32, "lt")
                self.B_lt = Buf()
                self.rstd = sb(es_, [128, 512], F32, "rstd")
                self.B_rstd = Buf()
                self.ss = ss if ss is not None else ps(es_, "ss")
                self.B_ss = B_ss if B_ss is not None else Buf()

        def load_h(ns, i, src, b, t512):
            sc.dma("sp", "hs%d" % i, ns.hs[i][:],
                   src[b][:, t512 * 512:(t512 + 1) * 512].rearrange("(k p) t -> p k t", p=128),
                   reads=[B_y[b][t512]], writes=[ns.B_hs[i]])

        def norm_a(ns, i):
            hs = ns.hs[i]
            sc.op("act", lambda h: h.activation(out=ns.sqb[:], in_=hs[:], func=AF.Square),
                  reads=[ns.B_hs[i]], writes=[ns.B_sqb])

        def norm_b1(ns, i):
            for k in range(8):
                mm(ns.ss[:], ones_bf, ns.sqb[:, k, :], k == 0, k == 7,
                   reads=[ns.B_sqb, B_cmat], writes=[ns.B_ss], sig=(k == 7))

        def norm_b2(ns, i):
            sc.op("act", lambda h: h.activation(out=ns.lt[:], in_=ns.ss[:], func=AF.Ln,
                                                scale=1.0 / D, bias=EPS),
                  reads=[ns.B_ss], writes=[ns.B_lt])
            sc.op("act", lambda h: h.activation(out=ns.rstd[:], in_=ns.lt[:], func=AF.Exp, scale=-0.5),
                  reads=[ns.B_lt], writes=[ns.B_rstd])

        def norm_c(ns, i, gcol, dest, B_dest, ks=range(8)):
            hs = ns.hs[i]
            for k in ks:
                sc.op("dve", lambda h, k=k: h.scalar_tensor_tensor(
                    out=dest[:, k, :], in0=hs[:, k, :], scalar=cst[:, gcol + k:gcol + k + 1],
                    in1=ns.rstd[:], op0=ALU.mult, op1=ALU.mult),
                    reads=[ns.B_hs[i], ns.B_rstd, B_cst], writes=[B_dest])

        def norm_sub(ns, i, gcol, dest, B_dest):
            norm_a(ns, i)
            norm_b1(ns, i)
            norm_b2(ns, i)
            norm_c(ns, i, gcol, dest, B_dest)

        def phase_ffn(l, i):
            src = state["src"]
            win_d = w_in[l, i].rearrange("(k p) n -> p k n", p=128)
            wout_d = w_out[l, i].rearrange("(j p) n -> p j n", p=128)
            gcol = C_GF + (l * 2 + i) * 8
            with contextlib.ExitStack() as pes:
                ns = NormScratch(pes, nhs=3)
                xn = [sb(pes, [128, 8, 1024], BF16, "xn") for _ in range(2)]
                B_xn = [[Buf(), Buf()], [Buf(), Buf()]]
                act = sb(pes, [128, 22, 1024], BF16, "act")
                B_act = [[Buf(), Buf()] for _ in range(22)]
                win = [sb(pes, [128, 8, 512], BF16, "win") for _ in range(3)]
                B_win = [[Buf(), Buf()] for _ in range(3)]
                wout = [sb(pes, [128, 22, 256], BF16, "wout") for _ in range(2)]
                B_wout = [Buf(), Buf()]
                sg = [sb(pes, [128, 512], F32, "sg") for _ in range(2)]
                B_sg = [Buf(), Buf()]
                gps = [ps(pes, "g") for _ in range(2)]
                ups = [ps(pes, "u") for _ in range(2)]
                ops_ = [ps(pes, "o") for _ in range(2)]
                B_g = [Buf(), Buf()]
                B_u = [Buf(), Buf()]
                B_o2 = [Buf(), Buf()]
                wc = 0
                oc = 0
                gc = 0
                uc = 0
                def norm_steps(tile):
                    b, t0 = tile // 4, (tile % 4) * 2
                    xnb = xn[tile % 2]
                    st = []
                    for s_ in range(2):
                        dest, Bd = xnb[:, :, s_ * 512:(s_ + 1) * 512], B_xn[tile % 2][s_]
                        st.append(lambda s_=s_, b=b, t0=t0: (load_h(ns, 2, src, b, t0 + s_), norm_a(ns, 2)))
                        st.append(lambda dest=dest, Bd=Bd: (norm_b1(ns, 2), norm_b2(ns, 2), norm_c(ns, 2, gcol, dest, Bd)))
                    return st

                for f in norm_steps(0):
                    f()
                for tile in range(NSEQ * 4):
                    b, t0 = tile // 4, (tile % 4) * 2
                    xnb = xn[tile % 2]
                    nxt = norm_steps(tile + 1) if tile + 1 < NSEQ * 4 else []
                    for pc in range(11):
                        if 4 <= pc < 4 + len(nxt):
                            nxt[pc - 4]()
                        sl = wc % 3
                        wc += 1
                        if (l, i, "in", pc) in B_wbf:
                            sc.dma("pool", "win%da" % sl, win[sl][:],
                                   winbf[l, i, pc].rearrange("p (k n) -> p k n", k=8),
                                   reads=[B_wbf[(l, i, "in", pc)]], writes=[B_win[sl][0], B_win[sl][1]])
                        else:
                            sc.dma("pool", "win%da" % sl, win[sl][:, :, 0:256],
                                   win_d[:, :, pc * 256:(pc + 1) * 256], writes=[B_win[sl][0]])
                            sc.dma("pool", "win%db" % sl, win[sl][:, :, 256:512],
                                   win_d[:, :, DFF + pc * 256:DFF + (pc + 1) * 256], writes=[B_win[sl][1]])
                            B_wbf[(l, i, "in", pc)] = Buf()
                            sc.dma("sp", "wsv%d" % sl, winbf[l, i, pc].rearrange("p (k n) -> p k n", k=8), win[sl][:],
                                   reads=B_win[sl], writes=[B_wbf[(l, i, "in", pc)]])
                        for jj in range(2):
                            j = pc * 2 + jj
                            for s_ in range(2):
                                gi = gc % 2
                                gc += 1
                                rhs_s = slice(s_ * 512, (s_ + 1) * 512)
                                for k in range(8):
                                    mm(gps[gi][:], win[sl][:, k, jj * 128:(jj + 1) * 128], xnb[:, k, rhs_s],
                                       k == 0, k == 7, reads=[B_win[sl][0], B_xn[tile % 2][s_]],
                                       writes=[B_g[gi]], sig=(k == 7))
                                for k in range(8):
                                    mm(ups[gi][:], win[sl][:, k, 256 + jj * 128:256 + (jj + 1) * 128],
                                       xnb[:, k, rhs_s], k == 0, k == 7,
                                       reads=[B_win[sl][1], B_xn[tile % 2][s_]], writes=[B_u[gi]], sig=(k == 7))
                                sc.op("act", lambda h, gi=gi: h.activation(out=sg[gi][:], in_=gps[gi][:], func=AF.Silu),
                                      reads=[B_g[gi]], writes=[B_sg[gi]])
                                sc.op("dve", lambda h, gi=gi, j=j, rhs_s=rhs_s: h.tensor_tensor(
                                    out=act[:, j, rhs_s], in0=sg[gi][:], in1=ups[gi][:], op=ALU.mult),
                                    reads=[B_sg[gi], B_u[gi]], writes=[B_act[j][s_]])
                    for s_ in range(2):
                        load_h(ns, s_, src, b, t0 + s_)
                    for np_ in range(4):
                        sl = oc % 2
                        oc += 1
                        if (l, i, "out", np_) in B_wbf:
                            sc.dma("pool", "wout%d" % sl, wout[sl][:],
                                   woutbf[l, i, np_].rearrange("p (j n) -> p j n", j=22),
                                   reads=[B_wbf[(l, i, "out", np_)]], writes=[B_wout[sl]])
                        else:
                            sc.dma("pool", "wout%d" % sl, wout[sl][:], wout_d[:, :, np_ * 256:(np_ + 1) * 256],
                                   writes=[B_wout[sl]])
                            B_wbf[(l, i, "out", np_)] = Buf()
                            sc.dma("sp", "wso%d" % sl, woutbf[l, i, np_].rearrange("p (j n) -> p j n", j=22), wout[sl][:],
                                   reads=[B_wout[sl]], writes=[B_wbf[(l, i, "out", np_)]])
                        for nn in range(2):
                            n = np_ * 2 + nn
                            for s_ in range(2):
                                oi = uc % 2
                                uc += 1
                                for j in range(22):
                                    mm(ops_[oi][:], wout[sl][:, j, nn * 128:(nn + 1) * 128],
                                       act[:, j, s_ * 512:(s_ + 1) * 512], j == 0, j == 21,
                                       reads=[B_wout[sl], B_act[j][s_]], writes=[B_o2[oi]], sig=(j == 21))
                                sc.op("dve", lambda h, oi=oi, n=n, s_=s_: h.scalar_tensor_tensor(
                                    out=ns.hs[s_][:, n, :], in0=ops_[oi][:], scalar=0.5, in1=ns.hs[s_][:, n, :],
                                    op0=ALU.mult, op1=ALU.add),
                                    reads=[B_o2[oi], ns.B_hs[s_]], writes=[ns.B_hs[s_]])
                    for s_ in range(2):
                        sc.dma("sp", "hs%d" % s_, yT[b][:, (t0 + s_) * 512:(t0 + s_ + 1) * 512].rearrange(
                            "(k p) t -> p k t", p=128), ns.hs[s_][:], reads=[ns.B_hs[s_]], writes=[B_y[b][t0 + s_]])
                sc.barrier()
            state["src"] = yT

        def run_pipeline(units, extras=None):
            extras = sorted(extras or [], key=lambda t: t[0])
            ei = 0
            maxst = max(len(u) for u in units) if units else 0
            for i in range(len(units) + max(maxst - 1, 0)):
                while ei < len(extras) and extras[ei][0] <= i:
                    extras[ei][1]()
                    ei += 1
                for s in range(maxst):
                    ui_ = i - s
                    if 0 <= ui_ < len(units) and s < len(units[ui_]):
                        units[ui_][s]()
            while ei < len(extras):
                extras[ei][1]()
                ei += 1

        def phase_proj(kind):
            src = state["src"]
            with contextlib.ExitStack() as pes:
                NBIG = 4
                big = [ps(pes, "big") for _ in range(NBIG)]
                B_big = [Buf() for _ in range(NBIG)]
                ns = NormScratch(pes, ss=big[NBIG - 1], B_ss=B_big[NBIG - 1])
                hn = sb(pes, [128, 8, 2048], BF16, "hn")
                B_hn = [Buf() for _ in range(4)]
                NWQ = 3
                wq = [sb(pes, [128, 8, 128], BF16, "wq") for _ in range(NWQ)]
                B_wq = [Buf() for _ in range(NWQ)]
                wv = [sb(pes, [128, 8, 512], BF16, "wv") for _ in range(2)]
                B_wv = [Buf(), Buf()]

                def two(shape, dtype, name):
                    return [sb(pes, shape, dtype, name) for _ in range(2)], [Buf(), Buf()]
                sqs, B_sqs = two([128, 512], BF16, "sqs")
                lt2, B_lt2 = two([128, 512], F32, "lt2")
                rs2, B_rs2 = two([128, 512], F32, "rs2")
                qn, B_qn = two([128, 512], BF16, "qn")
                t1, B_t1 = two([128, 512], F32, "t1")
                t2, B_t2 = two([128, 512], F32, "t2")
                qst, B_qst = two([128, 2048], BF16, "qst")
                vst, B_vst = two([128, 512], BF16, "vst")
                ss2 = [ps(pes, "ss2") for _ in range(2)]
                B_ss2 = [Buf(), Buf()]
                sw = [ps(pes, "sw") for _ in range(2)]
                B_sw = [Buf(), Buf()]
                if kind == "A":
                    ctab = sb(pes, [128, 2048], F32, "ctab")
                    stab = sb(pes, [128, 2048], F32, "stab")
                    pi = sb(pes, [128, 512], I32, "pi")
                    u = sb(pes, [128, 512], F32, "u")
                    ui = sb(pes, [128, 512], I32, "ui")
                    uf = sb(pes, [128, 512], F32, "uf")
                    tmpf = sb(pes, [128, 512], F32, "tmpf")
                    B_ct, B_st, B_pi, B_u, B_ui, B_uf, B_tmpf = Buf(), Buf(), Buf(), Buf(), Buf(), Buf(), Buf()
                if kind == "KV":
                    wf = sb(pes, [128, 8, 16], BF16, "wf")
                    B_wf = Buf()
                    lst = sb(pes, [16, 2048], F32, "lst")
                    B_lst = Buf()
                    ef, B_ef = two([16, 512], F32, "ef")
                    sc.dma("pool", "wf", wf[:], kv_w.rearrange("(k p) n -> p k n", p=128)[:, :, 2048:2064],
                           writes=[B_wf])
                gmcol = {"A": C_GM, "KV": C_GKV, "QB": C_GM + 8}[kind]
                cnt = {"u": 0, "wv": 0, "vs": 0, "qs": 0, "wq": 0}

                def trig(dst, B_dst, shift):
                    sc.op("dve", lambda h: h.tensor_scalar(out=uf[:], in0=u[:], scalar1=float(shift), scalar2=None,
                                                           op0=ALU.add), reads=[B_u], writes=[B_uf])
                    sc.op("dve", lambda h: h.tensor_copy(out=ui[:], in_=uf[:]), reads=[B_uf], writes=[B_ui])
                    sc.op("dve", lambda h: h.tensor_copy(out=tmpf[:], in_=ui[:]), reads=[B_ui], writes=[B_tmpf])
                    sc.op("dve", lambda h: h.tensor_tensor(out=uf[:], in0=uf[:], in1=tmpf[:], op=ALU.subtract),
                          reads=[B_uf, B_tmpf], writes=[B_uf])
                    sc.op("dve", lambda h: h.tensor_single_scalar(out=tmpf[:], in_=uf[:], scalar=0.5, op=ALU.is_gt),
                          reads=[B_uf], writes=[B_tmpf])
                    sc.op("dve", lambda h: h.tensor_tensor(out=uf[:], in0=uf[:], in1=tmpf[:], op=ALU.subtract),
                          reads=[B_uf, B_tmpf], writes=[B_uf])
                    sc.op("dve", lambda h: h.tensor_single_scalar(out=tmpf[:], in_=uf[:], scalar=-0.5, op=ALU.is_lt),
                          reads=[B_uf], writes=[B_tmpf])
                    sc.op("dve", lambda h: h.tensor_tensor(out=uf[:], in0=uf[:], in1=tmpf[:], op=ALU.add),
                          reads=[B_uf, B_tmpf], writes=[B_uf])
                    sc.op("dve", lambda h: h.tensor_scalar(out=uf[:], in0=uf[:], scalar1=0.49999, scalar2=-0.49999,
                                                           op0=ALU.min, op1=ALU.max), reads=[B_uf], writes=[B_uf])
                    sc.op("act", lambda h: h.activation(out=dst, in_=uf[:], func=AF.Sin, scale=2.0 * math.pi),
                          reads=[B_uf], writes=[B_dst])

                def load_wq(ch):
                    w_d, col0 = ch["w"], ch["col0"]
                    sl = ch["wslot"]
                    sc.dma("pool", "wq%d" % sl, wq[sl][:], w_d[:, :, col0:col0 + 128], writes=[B_wq[sl]])

                def qk_units(chunks, t0):
                    units = []
                    for ci, ch in enumerate(chunks):
                        ch["wslot"] = cnt["wq"] % NWQ
                        cnt["wq"] += 1
                        ch["qs"] = cnt["qs"] % 2
                        cnt["qs"] += 1
                    for ci, ch in enumerate(chunks):
                        for st_ in range(4):
                            un = cnt["u"]
                            cnt["u"] += 1
                            bi_, x2 = un % NBIG, un % 2
                            acc = big[bi_]
                            sl, qs, d, rope, gaincol = ch["wslot"], ch["qs"], ch["d"], ch["rope"], ch["gain"]
                            cs, ce = st_ * 512, (st_ + 1) * 512

                            def s0(ci=ci, st_=st_, acc=acc, bi_=bi_, sl=sl):
                                if st_ == 0 and ci == 0:
                                    load_wq(chunks[0])
                                if st_ == 0 and ci + 1 < len(chunks):
                                    load_wq(chunks[ci + 1])
                                for k in range(8):
                                    mm(acc[:], wq[sl][:, k, :], hn[:, k, st_ * 512:(st_ + 1) * 512], k == 0, k == 7,
                                       reads=[B_wq[sl], B_hn[st_]], writes=[B_big[bi_]], sig=(k == 7))

                            def s1(acc=acc, bi_=bi_, x2=x2):
                                sc.op("act", lambda h: h.activation(out=sqs[x2][:], in_=acc[:], func=AF.Square),
                                      reads=[B_big[bi_]], writes=[B_sqs[x2]])
                                mm(ss2[x2][:], bd_ones, sqs[x2][:], True, True, reads=[B_sqs[x2], B_cmat],
                                   writes=[B_ss2[x2]], sig=True)

                            def s2a(x2=x2):
                                sc.op("act", lambda h: h.activation(out=lt2[x2][:], in_=ss2[x2][:], func=AF.Ln,
                                                                    scale=1.0 / 64, bias=EPS),
                                      reads=[B_ss2[x2]], writes=[B_lt2[x2]])
                                sc.op("act", lambda h: h.activation(out=rs2[x2][:], in_=lt2[x2][:], func=AF.Exp, scale=-0.5),
                                      reads=[B_lt2[x2]], writes=[B_rs2[x2]])

                            def s2(acc=acc, bi_=bi_, x2=x2, rope=rope, gaincol=gaincol, qs=qs, cs=cs, ce=ce):
                                if not rope:
                                    sc.op("dve", lambda h: h.scalar_tensor_tensor(
                                        out=qst[qs][:, cs:ce], in0=acc[:], scalar=cst[:, gaincol:gaincol + 1], in1=rs2[x2][:],
                                        op0=ALU.mult, op1=ALU.mult),
                                        reads=[B_big[bi_], B_rs2[x2], B_cst], writes=[B_qst[qs]])
                                    return
                                sc.op("dve", lambda h: h.scalar_tensor_tensor(
                                    out=qn[x2][:], in0=acc[:], scalar=cst[:, gaincol:gaincol + 1], in1=rs2[x2][:],
                                    op0=ALU.mult, op1=ALU.mult),
                                    reads=[B_big[bi_], B_rs2[x2], B_cst], writes=[B_qn[x2]])
                                mm(sw[x2][:], pm, qn[x2][:], True, True, reads=[B_qn[x2], B_cmat], writes=[B_sw[x2]], sig=True)

                            def s3(x2=x2, qs=qs, d=d, st_=st_, cs=cs, ce=ce, ch=ch, t0=t0, rope=rope):
                                if rope:
                                    sc.op("pool", lambda h: h.tensor_tensor(out=t1[x2][:], in0=qn[x2][:], in1=ctab[:, cs:ce],
                                                                            op=ALU.mult),
                                          reads=[B_qn[x2], B_ct], writes=[B_t1[x2]])
                                    sc.op("dve", lambda h: h.tensor_tensor(out=t2[x2][:], in0=sw[x2][:], in1=stab[:, cs:ce],
                                                                           op=ALU.mult),
                                          reads=[B_sw[x2], B_st], writes=[B_t2[x2]])
                                    m_per = 512 // d
                                    o_ap = qst[qs][:, :].rearrange("p (r m) -> p r m", r=d)[:, :, st_ * m_per:(st_ + 1) * m_per]
                                    i1 = t1[x2][:, :].rearrange("p (m r) -> p r m", r=d)
                                    i2 = t2[x2][:, :].rearrange("p (m r) -> p r m", r=d)
                                    sc.op("pool", lambda h: h.tensor_tensor(out=o_ap, in0=i1, in1=i2, op=ALU.add),
                                          reads=[B_t1[x2], B_t2[x2]], writes=[B_qst[qs]])
                                if st_ == 3:
                                    dv = ch["dest"].rearrange("p (r m) -> p r m", r=d)[:, :, t0 // d:(t0 + 2048) // d]
                                    sc.dma("sp", "qst%d" % qs, dv, qst[qs][:, :].rearrange("p (r m) -> p r m", r=d),
                                           reads=[B_qst[qs]], writes=[ch["B_dest"]])
                            units.append([s0, s1, s2a, s2, s3])
                    return units

                def v_units(w_d, col0, d, dest, Bdict, bkey, t0):
                    units = []
                    nb = 32 // d
                    nml = 2048 // (128 * d)
                    for vh in range(2):
                        sl = cnt["wv"] % 2
                        cnt["wv"] += 1
                        first = [True]
                        for r in range(d):
                            for ml in range(nml):
                                bi = r * nb + t0 // (128 * d) + ml
                                un = cnt["u"]
                                cnt["u"] += 1
                                bi_ = un % NBIG
                                st0 = 128 * ml * d + r
                                vs = cnt["vs"] % 2
                                cnt["vs"] += 1
                                ld = (r == 0 and ml == 0)

                                def s0(sl=sl, vh=vh, bi_=bi_, st0=st0, ld=ld):
                                    if ld:
                                        sc.dma("pool", "wv%d" % sl, wv[sl][:], w_d[:, :, col0 + vh * 512:col0 + (vh + 1) * 512],
                                               writes=[B_wv[sl]])
                                    for k in range(8):
                                        lhs = hn[:, k, st0:st0 + 127 * d + 1:d] if d > 1 else hn[:, k, st0:st0 + 128]
                                        mm(big[bi_][:], lhs, wv[sl][:, k, :], k == 0, k == 7,
                                           reads=[B_wv[sl]] + B_hn, writes=[B_big[bi_]], sig=(k == 7))

                                def s1(bi_=bi_, vs=vs, bi=bi, vh=vh):
                                    if vs == 0:
                                        sc.op("act", lambda h: h.activation(out=vst[vs][:], in_=big[bi_][:], func=AF.Copy),
                                              reads=[B_big[bi_]], writes=[B_vst[vs]])
                                    else:
                                        sc.op("dve", lambda h: h.tensor_copy(out=vst[vs][:], in_=big[bi_][:]),
                                              reads=[B_big[bi_]], writes=[B_vst[vs]])
                                    sc.dma("sp", "vst%d" % vs, dest[bi][:, vh * 512:(vh + 1) * 512], vst[vs][:],
                                           reads=[B_vst[vs]], writes=[gb(Bdict, (bkey, bi, vh))])
                                units.append([s0, s1])
                    return units

                for b in range(NSEQ):
                    for half in range(2):
                        t0 = half * 2048
                        for st_ in range(4):
                            load_h(ns, st_ % 2, src, b, half * 4 + st_)
                            norm_sub(ns, st_ % 2, gmcol, hn[:, :, st_ * 512:(st_ + 1) * 512], B_hn[st_])
                        units = []
                        if kind == "A":
                            for pc_ in range(4):
                                sc.dma("sp", "pi", pi[:], posb[b][:, t0 + pc_ * 512:t0 + (pc_ + 1) * 512], writes=[B_pi])
                                sc.op("dve", lambda h: h.tensor_copy(out=u[:], in_=pi[:]), reads=[B_pi], writes=[B_u])
                                sc.op("dve", lambda h: h.tensor_scalar(out=u[:], in0=u[:], scalar1=cst[:, C_INVF:C_INVF + 1],
                                                                       scalar2=None, op0=ALU.mult),
                                      reads=[B_u, B_cst], writes=[B_u])
                                trig(stab[:, pc_ * 512:(pc_ + 1) * 512], B_st, 0.0)
                                trig(ctab[:, pc_ * 512:(pc_ + 1) * 512], B_ct, 0.25)
                            aqkv = a_qkv.rearrange("(k p) n -> p k n", p=128)
                            for g in range(3):
                                chunks = []
                                for c in range(2):
                                    for pair in range(8):
                                        chunks.append(dict(w=aqkv, col0=g * 3072 + c * 1024 + pair * 128,
                                                           gain=(C_AQ if c == 0 else C_AK) + g, rope=True, d=DIL[g],
                                                           dest=qk_scr[b, g, c, pair], B_dest=gb(B_qk, (b, g, c, pair))))
                                units += qk_units(chunks, t0)
                                units += v_units(aqkv, g * 3072 + 2048, DIL[g], v_scr[b, g], B_v, (b, g), t0)
                            run_pipeline(units)
                        elif kind == "KV":
                            kvw = kv_w.rearrange("(k p) n -> p k n", p=128)
                            chunks = [dict(w=kvw, col0=pair * 128, gain=C_KVK, rope=False, d=1, dest=kB_scr[b, pair],
                                           B_dest=gb(B_qk, ("kB", b, pair))) for pair in range(8)]
                            units += qk_units(chunks, t0)
                            units += v_units(kvw, 1024, 1, vB_scr[b], B_v, ("B", b), t0)
                            for st_ in range(4):
                                un = cnt["u"]
                                cnt["u"] += 1
                                bi_, x2 = un % NBIG, un % 2

                                def s0(bi_=bi_, st_=st_):
                                    for k in range(8):
                                        mm(big[bi_][0:16, :], wf[:, k, :], hn[:, k, st_ * 512:(st_ + 1) * 512], k == 0, k == 7,
                                           reads=[B_wf, B_hn[st_]], writes=[B_big[bi_]], sig=(k == 7))

                                def s1(bi_=bi_, x2=x2, st_=st_):
                                    sc.op("act", lambda h: h.activation(out=ef[x2][:], in_=big[bi_][0:16, :], func=AF.Exp,
                                                                        scale=-1.0, bias=nbf[0:16, :]),
                                          reads=[B_big[bi_], B_nbf], writes=[B_ef[x2]])
                                    sc.op("act", lambda h: h.activation(out=lst[:, st_ * 512:(st_ + 1) * 512], in_=ef[x2][:],
                                                                        func=AF.Ln, scale=1.0, bias=1.0),
                                          reads=[B_ef[x2]], writes=[B_lst])
                                units.append([s0, s1])
                            run_pipeline(units)
                            sc.dma("sp", "lst", lf_scr[b][:, t0:t0 + 2048], lst[:], reads=[B_lst],
                                   writes=[gb(B_misc, ("lf", b, half))])
                        else:
                            bwq = b_wq.rearrange("(k p) n -> p k n", p=128)
                            chunks = [dict(w=bwq, col0=pair * 128, gain=C_BQ, rope=False, d=1, dest=qB_scr[b, pair],
                                           B_dest=gb(B_qk, ("qB", b, pair))) for pair in range(8)]
                            units += qk_units(chunks, t0)
                            run_pipeline(units)
                sc.barrier()

        def phase_cum():
            with contextlib.ExitStack() as pes:
                lin = sb(pes, [16, S], F32, "lin")
                zer = sb(pes, [16, S], F32, "zer")
                lc = sb(pes, [16, S], F32, "lc")
                r1 = sb(pes, [16, S], F32, "r1")
                tf = sb(pes, [16, S], F32, "tf")
                pos3 = [sb(pes, [16, S], BF16, "p3") for _ in range(3)]
                neg3 = [sb(pes, [16, S], BF16, "n3") for _ in range(3)]
                one = sb(pes, [16, S], BF16, "one")
                B_lin, B_z, B_lc, B_r1, B_tf, B_one = Buf(), Buf(), Buf(), Buf(), Buf(), Buf()
                B_p3 = [Buf() for _ in range(3)]
                B_n3 = [Buf() for _ in range(3)]
                sc.op("pool", lambda h: h.memset(zer[:], 0.0), writes=[B_z])
                sc.op("pool", lambda h: h.memset(one[:], 1.0), writes=[B_one])
                for b in range(NSEQ):
                    sc.dma("sp", "lin", lin[:], lf_scr[b], reads=[gb(B_misc, ("lf", b, 0)), gb(B_misc, ("lf", b, 1))],
                           writes=[B_lin])
                    sc.op("dve", lambda h: h.tensor_scalar(out=lin[:], in0=lin[:], scalar1=8.0, scalar2=None, op0=ALU.mult),
                          reads=[B_lin], writes=[B_lin])
                    sc.op("dve", lambda h: h.tensor_tensor_scan(out=lc[:], data0=lin[:], data1=zer[:], initial=0.0,
                                                                op0=ALU.add, op1=ALU.add),
                          reads=[B_lin, B_z], writes=[B_lc])
                    cur, Bcur = lc, B_lc
                    for i in range(3):
                        sc.op("dve", lambda h, i=i, cur=cur: h.tensor_copy(out=pos3[i][:], in_=cur[:]),
                              reads=[Bcur], writes=[B_p3[i]])
                        sc.op("dve", lambda h, i=i: h.tensor_scalar(out=neg3[i][:], in0=pos3[i][:], scalar1=-1.0,
                                                                    scalar2=None, op0=ALU.mult),
                              reads=[B_p3[i]], writes=[B_n3[i]])
                        if i < 2:
                            sc.op("dve", lambda h, i=i: h.tensor_copy(out=tf[:], in_=pos3[i][:]),
                                  reads=[B_p3[i]], writes=[B_tf])
                            sc.op("dve", lambda h, cur=cur: h.tensor_tensor(out=r1[:], in0=cur[:], in1=tf[:], op=ALU.subtract),
                                  reads=[Bcur, B_tf], writes=[B_r1])
                            cur, Bcur = r1, B_r1
                    Bq = gb(B_misc, ("qb", b))
                    Bk = gb(B_misc, ("kb", b))
                    for i in range(3):
                        sc.dma("sp", "cq%d" % i, qb_scr[b][:, i, :], neg3[i][:], reads=[B_n3[i]], writes=[gb(B_misc, ("qb", b, i))])
                        sc.dma("sp", "ck%d" % i, kb_scr[b][:, 3 + i, :], pos3[i][:], reads=[B_p3[i]], writes=[gb(B_misc, ("kb", b, i))])
                        sc.dma("sp", "co%d" % i, qb_scr[b][:, 3 + i, :], one[:], reads=[B_one], writes=[gb(B_misc, ("qb1", b, i))])
                        sc.dma("sp", "cp%d" % i, kb_scr[b][:, i, :], one[:], reads=[B_one], writes=[gb(B_misc, ("kb1", b, i))])
                sc.barrier()

        def phase_attn(kind):
            with contextlib.ExitStack() as pes:
                if kind == "A":
                    acc = [sb(pes, [128, S], F32, "acc") for _ in range(2)]
                    B_acc = [Buf(), Buf()]
                NQS = 2 if kind == "A" else 1
                qH = [[sb(pes, [128, S], BF16, "qH") for _ in range(2)] for _ in range(NQS)]
                B_q = [[Buf(), Buf()] for _ in range(NQS)]
                qS = sb(pes, [128, S], BF16, "qS")
                B_qS = Buf()
                if kind == "A":
                    kT = [sb(pes, [128, S], BF16, "kT") for _ in range(2)]
                    B_k1 = [Buf(), Buf()]
                else:
                    kH = [[sb(pes, [128, S], BF16, "kH") for _ in range(2)]]
                    B_k = [[Buf(), Buf()]]
                    kS = sb(pes, [128, S], BF16, "kS")
                    B_kS = Buf()
                    bst = sb(pes, [6, S], BF16, "bst")
                    B_bst = Buf()
                V2 = [[sb(pes, [128, 32, 128], BF16, "V") for _ in range(2)] for _ in range(2)]
                B_V = [[Buf(), Buf()], [Buf(), Buf()]]
                B_V1 = Buf()
                op_ = [sb(pes, [128, S], BF16, "opair") for _ in range(2)]
                B_opc = [[Buf() for _ in range(8)] for _ in range(2)]
                opc = [0]
                if kind == "A":
                    dd = sb(pes, [128, S], F32, "dd")
                    B_ddc = [Buf() for _ in range(8)]
                else:
                    rdt = [sb(pes, [128, 512], F32, "rdt") for _ in range(2)]
                    B_rdt = [Buf(), Buf()]
                NP = 4
                pt = [sb(pes, [128, 512], BF16, "pt") for _ in range(NP)]
                B_pt = [Buf() for _ in range(NP)]
                stp = [ps(pes, "st") for _ in range(NP)]
                B_stp = [Buf() for _ in range(NP)]
                accp = [ps(pes, "ap") for _ in range(2)]
                B_accp = [Buf(), Buf()]
                for sl in range(2):
                    sc.op("pool", lambda h, sl=sl: h.memset(V2[sl][0][:, :, 64:128], 1.0), writes=[B_V1])
                    sc.op("pool", lambda h, sl=sl: h.memset(V2[sl][1][:, :, 0:64], 1.0), writes=[B_V1])
                cnt = {"ld": 0, "st": 0, "ap": 0, "hb": 0}

                def finalize(b, pair):
                    oi = opc[0] % 2
                    opc[0] += 1
                    for c in range(8):
                        cs = slice(c * 512, (c + 1) * 512)
                        sc.op("dve", lambda h, cs=cs: h.tensor_copy(out=dd[0:64, cs], in_=acc[0][64:128, cs]),
                              reads=[B_acc[0]], writes=[B_ddc[c]])
                        sc.op("dve", lambda h, cs=cs: h.tensor_copy(out=dd[64:128, cs], in_=acc[1][0:64, cs]),
                              reads=[B_acc[1]], writes=[B_ddc[c]])
                        sc.op("act", lambda h, cs=cs: h.activation(out=dd[:, cs], in_=dd[:, cs], func=AF.Ln),
                              reads=[B_ddc[c]], writes=[B_ddc[c]])
                        sc.op("act", lambda h, cs=cs: h.activation(out=dd[:, cs], in_=dd[:, cs], func=AF.Exp, scale=-1.0),
                              reads=[B_ddc[c]], writes=[B_ddc[c]])
                    for c in range(8):
                        cs = slice(c * 512, (c + 1) * 512)
                        sc.op("pool", lambda h, cs=cs, oi=oi: h.tensor_tensor(out=op_[oi][0:64, cs], in0=acc[0][0:64, cs],
                                                                              in1=dd[0:64, cs], op=ALU.mult),
                              reads=[B_acc[0], B_ddc[c]], writes=[B_opc[oi][c]])
                        sc.op("dve", lambda h, cs=cs, oi=oi: h.tensor_tensor(out=op_[oi][64:128, cs], in0=acc[1][64:128, cs],
                                                                             in1=dd[64:128, cs], op=ALU.mult),
                              reads=[B_acc[1], B_ddc[c]], writes=[B_opc[oi][c]])
                    sc.dma("sp", "opair%d" % oi, o_scr[b][pair * 128:(pair + 1) * 128, :], op_[oi][:], reads=B_opc[oi],
                           writes=[B_o[b][pair]])

                def mask_copy(e, dst, src_, col, reads, writes):
                    sc.op(e, lambda h: h.tensor_scalar(out=dst[:], in0=src_[:], scalar1=cst[:, col:col + 1], scalar2=None,
                                                       op0=ALU.mult), reads=reads + [B_cst], writes=writes)

                def prepare(j):
                    b, pair, g = jobs[j]
                    qs_ = j % NQS
                    mask_copy("dve", qH[qs_][0], qS, C_MA, [B_qS], [B_q[qs_][0]])
                    mask_copy("dve", qH[qs_][1], qS, C_MB, [B_qS], [B_q[qs_][1]])
                    if kind == "B":
                        mask_copy("dve", kH[0][0], kS, C_MA, [B_kS], [B_k[0][0]])
                        mask_copy("dve", kH[0][1], kS, C_MB, [B_kS], [B_k[0][1]])
                        for hd in range(2):
                            hh = pair * 2 + hd
                            r0 = 64 if hd == 0 else 0
                            rdq = [gb(B_misc, ("qb", b, i)) for i in range(3)] + [gb(B_misc, ("qb1", b, i)) for i in range(3)]
                            rdk = [gb(B_misc, ("kb", b, i)) for i in range(3)] + [gb(B_misc, ("kb1", b, i)) for i in range(3)]
                            for (srcd, rd_, dst, Bd) in ((qb_scr[b, hh], rdq, qH[0][hd], B_q[0][hd]),
                                                         (kb_scr[b, hh], rdk, kH[0][hd], B_k[0][hd])):
                                sc.dma("sp", "bst", bst[:], srcd, reads=rd_, writes=[B_bst])
                                sc.op("dve", lambda h, dst=dst, r0=r0: h.tensor_copy(out=dst[r0:r0 + 6, :], in_=bst[:]),
                                      reads=[B_bst], writes=[Bd])

                def load_group(sl, b, pair, qsrc, ksrc, vsrc, Bq_d, Bk_d, vkey):
                    sc.dma("sp", "qS", qS[:], qsrc, reads=[Bq_d], writes=[B_qS])
                    if kind == "A":
                        sc.dma("sp", "kT%d" % sl, kT[sl][:], ksrc, reads=[Bk_d], writes=[B_k1[sl]])
                    else:
                        sc.dma("sp", "kS", kS[:], ksrc, reads=[Bk_d], writes=[B_kS])
                    vr = vsrc.rearrange("bi i c -> i bi c")
                    rd = [gb(B_v, (vkey, bi, pair // 4)) for bi in range(32)]
                    sc.dma("sp", "VA%d" % sl, V2[sl][0][:, :, 0:64], vr[:, :, pair * 128:pair * 128 + 64],
                           reads=rd, writes=[B_V[sl][0]])
                    sc.dma("sp", "VB%d" % sl, V2[sl][1][:, :, 64:128], vr[:, :, pair * 128 + 64:pair * 128 + 128],
                           reads=rd, writes=[B_V[sl][1]])
                    return sl

                def exp_stage(si, n):
                    def f():
                        sc.op("act", lambda h: h.activation(out=pt[si][:, 0:n], in_=stp[si][:, 0:n], func=AF.Exp,
                                                            scale=0.125),
                              reads=[B_stp[si]], writes=[B_pt[si]])
                    return f

                if kind == "A":
                    jobs = [(b, pair, g) for b in range(NSEQ) for pair in range(8) for g in range(3)]
                else:
                    jobs = [(b, pair, 0) for b in range(NSEQ) for pair in range(8)]

                def issue_load(j):
                    if j >= len(jobs):
                        return
                    b, pair, g = jobs[j]
                    if kind == "A":
                        load_group(j % 2, b, pair, qk_scr[b, g, 0, pair], qk_scr[b, g, 1, pair], v_scr[b, g],
                                   gb(B_qk, (b, g, 0, pair)), gb(B_qk, (b, g, 1, pair)), (b, g))
                        prepare(j)
                    else:
                        load_group(j % 2, b, pair, qB_scr[b, pair], kB_scr[b, pair], vB_scr[b],
                                   gb(B_qk, ("qB", b, pair)), gb(B_qk, ("kB", b, pair)), ("B", b))

                conv = []
                if kind == "A":
                    cv = [sb(pes, [128, 5632], BF16, "cv") for _ in range(2)]
                    B_cv = [[Buf(), Buf()], [Buf(), Buf()]]
                    for (l_, i_) in ((0, 1), (1, 0), (1, 1)):
                        conv += [(l_, i_, "in", pc_) for pc_ in range(11)] + [(l_, i_, "out", n_) for n_ in range(4)]
                cvn = [0]

                def conv_extras(steps):
                    ex = []
                    for (s_ld, s_st) in steps:
                        if cvn[0] >= len(conv):
                            break
                        l_, i_, kd, idx = conv[cvn[0]]
                        sl_ = cvn[0] % 2
                        cvn[0] += 1
                        key = (l_, i_, kd, idx)
                        B_wbf[key] = Buf()
                        if kd == "in":
                            wd = w_in[l_, i_].rearrange("(k p) n -> p k n", p=128)
                            v3 = cv[sl_][:, 0:4096].rearrange("p (k n) -> p k n", k=8)

                            def ld(wd=wd, v3=v3, idx=idx, sl_=sl_):
                                sc.dma("pool", "cv%da" % sl_, v3[:, :, 0:256], wd[:, :, idx * 256:(idx + 1) * 256],
                                       writes=[B_cv[sl_][0]])
                                sc.dma("pool", "cv%db" % sl_, v3[:, :, 256:512],
                                       wd[:, :, DFF + idx * 256:DFF + (idx + 1) * 256], writes=[B_cv[sl_][1]])

                            def st(l_=l_, i_=i_, idx=idx, sl_=sl_, key=key):
                                sc.dma("sp", "cvs%d" % sl_, winbf[l_, i_, idx], cv[sl_][:, 0:4096],
                                       reads=B_cv[sl_], writes=[B_wbf[key]])
                        else:
                            wd = w_out[l_, i_].rearrange("(j p) n -> p j n", p=128)
                            v3 = cv[sl_][:, :].rearrange("p (j n) -> p j n", j=22)

                            def ld(wd=wd, v3=v3, idx=idx, sl_=sl_):
                                sc.dma("pool", "cv%da" % sl_, v3, wd[:, :, idx * 256:(idx + 1) * 256],
                                       writes=[B_cv[sl_][0], B_cv[sl_][1]])

                            def st(l_=l_, i_=i_, idx=idx, sl_=sl_, key=key):
                                sc.dma("sp", "cvs%d" % sl_, woutbf[l_, i_, idx], cv[sl_][:, :],
                                       reads=B_cv[sl_], writes=[B_wbf[key]])
                        ex.append((s_ld, ld))
                        ex.append((s_st, st))
                    return ex

                issue_load(0)
                jobno = [0]
                for b in range(NSEQ):
                    for pair in range(8):
                        units = []
                        if kind == "A":
                            for hd in range(2):
                                sc.op("pool", lambda h, hd=hd: h.memset(acc[hd][:], 0.0), writes=[B_acc[hd]])
                            for g in range(3):
                                d = DIL[g]
                                nb = 32 // d
                                csz = min(4, nb)
                                jn = jobno[0]
                                jobno[0] += 1
                                sl = jn % 2
                                ucount = [0]
                                for hd in range(2):
                                    rb = hd * 64
                                    V = V2[sl][hd]
                                    for r in range(d):
                                        chunk_base = cnt["ap"]
                                        cnt["ap"] += nb // csz
                                        for m in range(nb):
                                            bi = r * nb + m
                                            base = bi * 128
                                            n = 256 if m + 1 < nb else 128
                                            si = cnt["st"] % NP
                                            cnt["st"] += 1

                                            pre = jn + 1 if ucount[0] == 6 else None
                                            ucount[0] += 1

                                            def sA(si=si, n=n, sl=sl, hd=hd, base=base, pre=pre):
                                                if pre is not None:
                                                    issue_load(pre)
                                                mm(stp[si][:, 0:n], kT[sl][:, base:base + 128],
                                                   qH[sl][hd][:, base:base + n], True, False,
                                                   reads=[B_k1[sl], B_q[sl][hd]], writes=[B_stp[si]], sig=False)
                                                mm(stp[si][:, 0:n], ident, band[:, 0:n], False, True,
                                                   reads=[B_cmat], writes=[B_stp[si]], sig=True)

                                            def sP(si=si, n=n, sl=sl, hd=hd, V=V, bi=bi, m=m, r=r, d=d, csz=csz,
                                                   chunk_base=chunk_base):
                                                ai = (chunk_base + m // csz) % 2
                                                col = (m % csz) * 128
                                                last_in_chunk = (m % csz == csz - 1)
                                                mm(accp[ai][:, col:col + 128], V[:, bi, :], pt[si][:, 0:128], m == 0, True,
                                                   reads=[B_pt[si], B_V[sl][hd], B_V1], writes=[B_accp[ai]],
                                                   sig=last_in_chunk)
                                                if last_in_chunk:
                                                    c0 = (m - csz + 1) * 128 * d + r
                                                    w = csz * 128
                                                    a_ap = acc[hd][:, c0:c0 + (w - 1) * d + 1:d] if d > 1 else acc[hd][:, c0:c0 + w]
                                                    sc.op("dve", lambda h: h.tensor_tensor(
                                                        out=a_ap, in0=accp[ai][:, 0:w], in1=a_ap, op=ALU.add),
                                                        reads=[B_accp[ai], B_acc[hd]], writes=[B_acc[hd]])
                                                if n == 256:
                                                    ai2 = (chunk_base + (m + 1) // csz) % 2
                                                    col2 = ((m + 1) % csz) * 128
                                                    mm(accp[ai2][:, col2:col2 + 128], V[:, bi, :], pt[si][:, 128:256], True, False,
                                                       reads=[B_pt[si], B_V[sl][hd], B_V1], writes=[B_accp[ai2]], sig=True)
                                            units.append([sA, exp_stage(si, n), sP])
                        else:
                            jn = jobno[0]
                            jobno[0] += 1
                            sl = jn % 2
                            ucount = [0]
                            prepare(jn)
                            oi = opc[0] % 2
                            opc[0] += 1
                            for hd in range(2):
                                hh = pair * 2 + hd
                                rb = hd * 64
                                V = V2[sl][hd]
                                for qt in range(8):
                                    ai = cnt["ap"] % 2
                                    cnt["ap"] += 1
                                    nkb = 4 * qt + 4
                                    for kb_ in range(nkb):
                                        j = kb_ - 4 * qt
                                        c0 = 128 * max(j, 0)
                                        n = 512 - c0
                                        q0 = qt * 512 + c0
                                        si = cnt["st"] % NP
                                        cnt["st"] += 1

                                        pre = jn + 1 if ucount[0] == 6 else None
                                        ucount[0] += 1

                                        def sA(si=si, n=n, sl=sl, hd=hd, kb_=kb_, q0=q0, j=j, pre=pre):
                                            if pre is not None:
                                                issue_load(pre)
                                            rds = [B_k[0][hd], B_q[0][hd]]
                                            mm(stp[si][:, 0:n], kH[0][hd][:, kb_ * 128:(kb_ + 1) * 128],
                                               qH[0][hd][:, q0:q0 + n], True, j < 0,
                                               reads=rds, writes=[B_stp[si]], sig=(j < 0))
                                            if j >= 0:
                                                mm(stp[si][:, 0:128], ident, caus, False, True,
                                                   reads=[B_cmat], writes=[B_stp[si]], sig=True)

                                        def sP(si=si, n=n, sl=sl, hd=hd, V=V, kb_=kb_, c0=c0, ai=ai, nkb=nkb, qt=qt, oi=oi):
                                            mm(accp[ai][:, c0:512], V[:, kb_, :], pt[si][:, 0:n], kb_ == 0, kb_ == nkb - 1,
                                               reads=[B_pt[si], B_V[sl][hd], B_V1], writes=[B_accp[ai]], sig=(kb_ == nkb - 1))
                                            if kb_ == nkb - 1:
                                                nr = slice(0, 64) if hd == 0 else slice(64, 128)
                                                dr = slice(64, 128) if hd == 0 else slice(0, 64)
                                                ri = (qt + hd) % 2
                                                sc.op("dve", lambda h: h.reciprocal(out=rdt[ri][nr, :], in_=accp[ai][dr, :]),
                                                      reads=[B_accp[ai]], writes=[B_rdt[ri]])
                                                sc.op("dve", lambda h: h.tensor_tensor(
                                                    out=op_[oi][nr, qt * 512:(qt + 1) * 512], in0=accp[ai][nr, :],
                                                    in1=rdt[ri][nr, :], op=ALU.mult),
                                                    reads=[B_accp[ai], B_rdt[ri]], writes=[B_opc[oi][qt]])
                                        units.append([sA, exp_stage(si, n), sP])
                        run_pipeline(units, conv_extras([(20, 50), (80, 110), (140, 170)]) if kind == "A" else None)
                        if kind == "A":
                            finalize(b, pair)
                        else:
                            sc.dma("sp", "opair%d" % oi, o_scr[b][pair * 128:(pair + 1) * 128, :], op_[oi][:], reads=B_opc[oi],
                                   writes=[B_o[b][pair]])
                sc.barrier()

        def phase_wo(w_d):
            src = state["src"]
            with contextlib.ExitStack() as pes:
                wo = sb(pes, [128, 8, D], BF16, "wo")
                B_wo = Buf()
                sc.dma("pool", "wo", wo[:], w_d.rearrange("(k p) n -> p k n", p=128), writes=[B_wo])
                hs = [sb(pes, [128, 8, 512], F32, "hsw") for _ in range(2)]
                B_hs = [Buf(), Buf()]
                ot = [sb(pes, [128, 8, 512], BF16, "ot") for _ in range(2)]
                B_ot = [Buf(), Buf()]
                pp = [ps(pes, "wo") for _ in range(2)]
                B_pp = [Buf(), Buf()]
                c = 0
                for b in range(NSEQ):
                    for t in range(8):
                        i = (b * 8 + t) % 2
                        sc.dma("sp", "hsw%d" % i, hs[i][:], src[b][:, t * 512:(t + 1) * 512].rearrange("(k p) t -> p k t", p=128),
                               reads=[B_y[b][t]], writes=[B_hs[i]])
                        sc.dma("sp", "ot%d" % i, ot[i][:], o_scr[b][:, t * 512:(t + 1) * 512].rearrange("(k p) t -> p k t", p=128),
                               reads=B_o[b], writes=[B_ot[i]])
                        for n in range(8):
                            pi_ = c % 2
                            c += 1
                            for k in range(8):
                                mm(pp[pi_][:], wo[:, k, n * 128:(n + 1) * 128], ot[i][:, k, :], k == 0, k == 7,
                                   reads=[B_wo, B_ot[i]], writes=[B_pp[pi_]], sig=(k == 7))
                            sc.op("dve", lambda h, pi_=pi_, i=i, n=n: h.tensor_tensor(
                                out=hs[i][:, n, :], in0=pp[pi_][:], in1=hs[i][:, n, :], op=ALU.add),
                                reads=[B_pp[pi_], B_hs[i]], writes=[B_hs[i]])
                        sc.dma("pool", "hsws%d" % i, yT[b][:, t * 512:(t + 1) * 512].rearrange("(k p) t -> p k t", p=128), hs[i][:],
                               reads=[B_hs[i]], writes=[B_y[b][t]])
                sc.barrier()
            state["src"] = yT

        phases = [
            lambda: phase_ffn(0, 0),
            lambda: phase_proj("A"),
            lambda: phase_attn("A"),
            lambda: phase_wo(a_wo),
            lambda: phase_ffn(0, 1),
            lambda: phase_proj("KV"),
            lambda: phase_cum(),
            lambda: phase_ffn(1, 0),
            lambda: phase_proj("QB"),
            lambda: phase_attn("B"),
            lambda: phase_wo(b_wo),
            lambda: phase_ffn(1, 1),
        ]
        for i, p in enumerate(phases):
            if i >= stop_after:
                break
            p()
        sc.barrier()
        with nc.Block() as block:
            sc.emit(block)
    return nc


def host_consts(inp):
    cst = np.zeros((128, NCST), np.float32)
    fn = np.asarray(inp["ffn_norm"], np.float32)
    for l in range(2):
        for i in range(2):
            cst[:, C_GF + (l * 2 + i) * 8:C_GF + (l * 2 + i) * 8 + 8] = fn[l, i].reshape(8, 128).T
    mn = np.asarray(inp["mix_norm"], np.float32)
    for l in range(2):
        cst[:, C_GM + l * 8:C_GM + l * 8 + 8] = mn[l].reshape(8, 128).T
    cst[:, C_GKV:C_GKV + 8] = np.asarray(inp["kv_norm"], np.float32).reshape(8, 128).T
    for g in range(3):
        cst[:, C_AQ + g] = np.tile(np.asarray(inp["a_q_norm"], np.float32)[0, g], 2)
        cst[:, C_AK + g] = np.tile(np.asarray(inp["a_k_norm"], np.float32)[0, g], 2)
    cst[:, C_KVK] = np.tile(np.asarray(inp["kv_k_norm"], np.float32), 2)
    cst[:, C_BQ] = np.tile(np.asarray(inp["b_q_norm"], np.float32)[0], 2)
    invf = (500000.0 ** (-np.arange(0, 16, 2, dtype=np.float64) / 16)) / (2 * np.pi)
    for p in range(128):
        j = p % 64
        cst[p, C_INVF] = invf[j % 8] if j < 16 else 0.0
    cst[0:16, C_BF] = np.asarray(inp["kv_b_f"], np.float32)
    cst[0:64, C_MA] = 1.0
    cst[64:128, C_MB] = 1.0
    cm = np.zeros((128, NMAT), np.float32)
    cm[:, M_ID:M_ID + 128] = np.eye(128)
    cm[:, M_ONES:M_ONES + 128] = 1.0
    for hb in range(2):
        cm[hb * 64:(hb + 1) * 64, M_BD + hb * 64:M_BD + (hb + 1) * 64] = 1.0
    for m in range(128):
        j = m % 64
        if j < 8:
            cm[m + 8, M_PM + m] = -1.0
        elif j < 16:
            cm[m - 8, M_PM + m] = 1.0
    kj = np.arange(128)[:, None]
    qc = np.arange(256)[None, :]
    cm[:, M_BAND:M_BAND + 256] = np.where((qc - kj >= 0) & (qc - kj <= 128), 0.0, NEG)
    qi = np.arange(128)[None, :]
    cm[:, M_CAUS:M_CAUS + 128] = np.where(kj <= qi, 0.0, NEG)
    return cst, cm


_CACHE = {}


def kernel(**inp):
    x = np.asarray(inp["x"], np.float32)
    pos = np.asarray(inp["positions"], np.int32)
    cst, cm = host_consts(inp)
    if "nc" not in _CACHE:
        _CACHE["nc"] = build_program()
    nc = _CACHE["nc"]
    shared = {
        "cst": cst, "cmat": cm,
        "ffn_w_in": np.ascontiguousarray(inp["ffn_w_in"], np.float32),
        "ffn_w_out": np.ascontiguousarray(inp["ffn_w_out"], np.float32),
        "a_w_qkv": np.ascontiguousarray(np.asarray(inp["a_w_qkv"], np.float32)[0]),
        "a_w_o": np.ascontiguousarray(np.asarray(inp["a_w_o"], np.float32)[0]),
        "kv_w": np.ascontiguousarray(inp["kv_w"], np.float32),
        "b_w_q": np.ascontiguousarray(np.asarray(inp["b_w_q"], np.float32)[0]),
        "b_w_o": np.ascontiguousarray(np.asarray(inp["b_w_o"], np.float32)[0]),
    }
    in_maps = []
    for c in range(NCORES):
        xs = x[c * NSEQ:(c + 1) * NSEQ]
        m = dict(shared)
        m["xT"] = np.ascontiguousarray(xs.transpose(0, 2, 1))
        m["posb"] = np.ascontiguousarray(np.broadcast_to(pos[c * NSEQ:(c + 1) * NSEQ, None, :], (NSEQ, 128, S)))
        in_maps.append(m)
    res = run_bass_kernel_spmd(nc, in_maps, core_ids=list(range(NCORES)))
    out = np.empty((NCORES * NSEQ, S, D), np.float32)
    for c in range(NCORES):
        out[c * NSEQ:(c + 1) * NSEQ] = np.asarray(res.results[c]["yT"]).transpose(0, 2, 1)
    return out
```
